# Optimizing a Trainium2 kernel written in Bass

```python
import math
import jax, jax.numpy as jnp
from jax import lax
import numpy as np

D_MODEL = 4096
BATCH = 2
SEQ = 8192
DEPTH = 2

EPS = 1e-6
GROUP = 128
W_MIX = D_MODEL // 4
N_BRANCH = 4
CHUNK = 128
CONV_A = 3
SGU_GROUPS = W_MIX // GROUP
SB_HEAD_DIM = 128
SB_HEADS = W_MIX // SB_HEAD_DIM
SSM_HEAD_DIM = 64
SSM_HEADS = W_MIX // SSM_HEAD_DIM
SSM_GROUPS = 2
SSM_HPG = SSM_HEADS // SSM_GROUPS
SSM_STATE = 128
SSM_CONV = 4
SSM_CONV_DIM = W_MIX + 2 * SSM_GROUPS * SSM_STATE
D_FF = -(-(8 * D_MODEL) // (3 * 256)) * 256
IN_SIZES = (W_MIX, W_MIX, W_MIX,
            W_MIX, W_MIX,
            W_MIX, W_MIX, W_MIX,
            W_MIX, SSM_CONV_DIM, SSM_HEADS,
            N_BRANCH * D_MODEL)
N_IN = sum(IN_SIZES)

kernel_name = "hybrid_gated_parallel_mixers"


def _split_points():
    return list(np.cumsum(np.array(IN_SIZES[:-1])))


def _rmsnorm(x, g):
    xf = x.astype(jnp.float32)
    y = xf * lax.rsqrt(jnp.mean(xf * xf, axis=-1, keepdims=True) + EPS)
    return (y * g.astype(jnp.float32)).astype(x.dtype)


def _causal_dwconv(x, w):
    K = w.shape[0]
    S = x.shape[1]
    xp = jnp.pad(x, ((0, 0), (K - 1, 0), (0, 0)))
    return sum(xp[:, k:k + S, :] * w[k] for k in range(K))


def _short_conv_mixer(b_g, c_g, xa, conv_w):
    return b_g * _causal_dwconv(c_g * xa, conv_w)


def _chunked_sgu(u, v, norm_g, w_s, b_s):
    Bsz, S, W = u.shape
    u = jax.nn.gelu(u)
    vf = jax.nn.gelu(v).astype(jnp.float32)
    mu = jnp.mean(vf, axis=-1, keepdims=True)
    var = jnp.mean(jnp.square(vf - mu), axis=-1, keepdims=True)
    vn = ((vf - mu) * lax.rsqrt(var + EPS) * norm_g.astype(jnp.float32)).astype(u.dtype)
    vn = vn.reshape(Bsz, S // CHUNK, CHUNK, SGU_GROUPS, GROUP)
    causal = jnp.tril(jnp.ones((CHUNK, CHUNK), dtype=bool))
    w = jnp.where(causal, w_s, jnp.zeros_like(w_s))
    mixed = jnp.einsum('gts,bnsgc->bntgc', w, vn) + b_s.T[None, None, :, :, None]
    return u * mixed.reshape(Bsz, S, W)


def _stick_breaking_attention(q, k, v, q_g, k_g):
    Bsz, S, W = q.shape
    f32 = jnp.float32
    q = _rmsnorm(q.reshape(Bsz, S, SB_HEADS, SB_HEAD_DIM), q_g).astype(f32)
    k = _rmsnorm(k.reshape(Bsz, S, SB_HEADS, SB_HEAD_DIM), k_g).astype(f32)
    v = v.reshape(Bsz, S, SB_HEADS, SB_HEAD_DIM).astype(f32)
    scale = 1.0 / math.sqrt(SB_HEAD_DIM)
    s_idx = jnp.arange(S)[None, :]

    def block(i):
        t0 = i * CHUNK
        qb = lax.dynamic_slice_in_dim(q, t0, CHUNK, axis=1)
        z = jnp.einsum('bthd,bshd->bhts', qb, k) * scale
        t_idx = t0 + jnp.arange(CHUNK)[:, None]
        mask = s_idx < t_idx
        log_beta = jax.nn.log_sigmoid(z)
        log_1m = jnp.where(mask, jax.nn.log_sigmoid(-z), 0.0)
        rev = lax.cumsum(log_1m, axis=3, reverse=True) - log_1m
        att = jnp.where(mask, jnp.exp(log_beta + rev), 0.0)
        return jnp.einsum('bhts,bshd->bthd', att, v)

    out = lax.map(block, jnp.arange(S // CHUNK))
    out = jnp.moveaxis(out, 0, 1).reshape(Bsz, S, W)
    return out


def _ssd_mixer(z, xbc, dt, conv_w, conv_b, dt_bias, a_log, d_skip, norm_g):
    Bsz, S, W = z.shape
    nc = S // CHUNK
    f32 = jnp.float32
    xbc = jax.nn.silu(_causal_dwconv(xbc, conv_w) + conv_b)
    xs, bm, cm = jnp.split(xbc, [W, W + SSM_GROUPS * SSM_STATE], axis=-1)
    xs = xs.astype(f32).reshape(Bsz, nc, CHUNK, SSM_GROUPS, SSM_HPG, SSM_HEAD_DIM)
    bm = bm.astype(f32).reshape(Bsz, nc, CHUNK, SSM_GROUPS, SSM_STATE)
    cm = cm.astype(f32).reshape(Bsz, nc, CHUNK, SSM_GROUPS, SSM_STATE)
    dt = jax.nn.softplus(dt.astype(f32) + dt_bias.astype(f32))
    dt = dt.reshape(Bsz, nc, CHUNK, SSM_GROUPS, SSM_HPG)
    a = -jnp.exp(a_log.astype(f32)).reshape(SSM_GROUPS, SSM_HPG)
    cum = jnp.cumsum(dt * a, axis=2)
    xdt = xs * dt[..., None]
    seg = cum[:, :, :, None] - cum[:, :, None, :]
    causal = jnp.tril(jnp.ones((CHUNK, CHUNK), dtype=bool))[:, :, None, None]
    decay = jnp.exp(jnp.where(causal, seg, -jnp.inf))
    cb = jnp.einsum('bctgn,bcsgn->bctsg', cm, bm)
    y_diag = jnp.einsum('bctsg,bctsgr,bcsgrp->bctgrp', cb, decay, xdt)
    dec_end = jnp.exp(cum[:, :, -1:] - cum)
    states = jnp.einsum('bcsgn,bcsgr,bcsgrp->bcgrpn', bm, dec_end, xdt)
    chunk_dec = jnp.exp(cum[:, :, -1])

    def step(h, inp):
        st, dc = inp
        return h * dc[..., None, None] + st, h

    h0 = jnp.zeros((Bsz, SSM_GROUPS, SSM_HPG, SSM_HEAD_DIM, SSM_STATE), f32)
    _, h_prev = lax.scan(step, h0, (jnp.moveaxis(states, 1, 0), jnp.moveaxis(chunk_dec, 1, 0)))
    h_prev = jnp.moveaxis(h_prev, 0, 1)
    y_off = jnp.einsum('bctgn,bctgr,bcgrpn->bctgrp', cm, jnp.exp(cum), h_prev)
    y = y_diag + y_off + xs * d_skip.astype(f32).reshape(SSM_GROUPS, SSM_HPG)[:, :, None]
    y = y.reshape(Bsz, S, W) * jax.nn.silu(z.astype(f32))
    y = y.reshape(Bsz, S, SSM_GROUPS, W // SSM_GROUPS)
    y = y * lax.rsqrt(jnp.mean(y * y, axis=-1, keepdims=True) + EPS)
    y = y * norm_g.astype(f32).reshape(SSM_GROUPS, W // SSM_GROUPS)
    return y.reshape(Bsz, S, W).astype(z.dtype)


def setup_inputs(seed: int = 0) -> dict:
    key = jax.random.key(seed)
    ks = jax.random.split(key, 24)
    f32 = jnp.float32

    def nrm(k, shape, scale):
        return jax.random.normal(k, shape, f32) * scale

    def gain(k, shape):
        return 1.0 + 0.02 * jax.random.normal(k, shape, f32)

    dt0 = jnp.exp(jax.random.uniform(ks[10], (DEPTH, SSM_HEADS), f32,
                                     minval=math.log(1e-3), maxval=math.log(1e-1)))
    return {
        "x": jax.random.normal(ks[0], (BATCH, SEQ, D_MODEL), f32),
        "norm_mix": gain(ks[1], (DEPTH, D_MODEL)),
        "w_in": nrm(ks[2], (DEPTH, D_MODEL, N_IN), D_MODEL ** -0.5),
        "conv_a": nrm(ks[3], (DEPTH, CONV_A, W_MIX), CONV_A ** -0.5),
        "sgu_norm": gain(ks[4], (DEPTH, W_MIX)),
        "sgu_w": nrm(ks[5], (DEPTH, SGU_GROUPS, CHUNK, CHUNK), CHUNK ** -0.5),
        "sgu_b": gain(ks[6], (DEPTH, SGU_GROUPS, CHUNK)),
        "q_norm": gain(ks[7], (DEPTH, SB_HEAD_DIM)),
        "k_norm": gain(ks[8], (DEPTH, SB_HEAD_DIM)),
        "ssm_conv_w": nrm(ks[9], (DEPTH, SSM_CONV, SSM_CONV_DIM), SSM_CONV ** -0.5),
        "ssm_conv_b": nrm(ks[11], (DEPTH, SSM_CONV_DIM), 0.02),
        "ssm_dt_bias": dt0 + jnp.log(-jnp.expm1(-dt0)),
        "ssm_a_log": jnp.log(jax.random.uniform(ks[12], (DEPTH, SSM_HEADS), f32, minval=1.0, maxval=16.0)),
        "ssm_d": gain(ks[13], (DEPTH, SSM_HEADS)),
        "ssm_norm": gain(ks[14], (DEPTH, W_MIX)),
        "w_branch": nrm(ks[15], (DEPTH, N_BRANCH, W_MIX, D_MODEL), W_MIX ** -0.5),
        "w_out": nrm(ks[16], (DEPTH, D_MODEL, D_MODEL), D_MODEL ** -0.5),
        "norm_ffn": gain(ks[17], (DEPTH, D_MODEL)),
        "w_ffn_gate": nrm(ks[18], (DEPTH, D_MODEL, D_FF), D_MODEL ** -0.5),
        "w_ffn_up": nrm(ks[19], (DEPTH, D_MODEL, D_FF), D_MODEL ** -0.5),
        "w_ffn_down": nrm(ks[20], (DEPTH, D_FF, D_MODEL), D_FF ** -0.5),
    }


def reference(x, norm_mix, w_in, conv_a, sgu_norm, sgu_w, sgu_b, q_norm, k_norm,
              ssm_conv_w, ssm_conv_b, ssm_dt_bias, ssm_a_log, ssm_d, ssm_norm,
              w_branch, w_out, norm_ffn, w_ffn_gate, w_ffn_up, w_ffn_down):
    Bsz, S, D = x.shape
    splits = _split_points()
    for i in range(DEPTH):
        h = _rmsnorm(x, norm_mix[i])
        proj = h @ w_in[i]
        (a_b, a_c, a_x, b_u, b_v, c_q, c_k, c_v,
         d_z, d_xbc, d_dt, gates) = jnp.split(proj, splits, axis=-1)
        y_a = _short_conv_mixer(a_b, a_c, a_x, conv_a[i])
        y_b = _chunked_sgu(b_u, b_v, sgu_norm[i], sgu_w[i], sgu_b[i])
        y_c = _stick_breaking_attention(c_q, c_k, c_v, q_norm[i], k_norm[i]).astype(x.dtype)
        y_d = _ssd_mixer(d_z, d_xbc, d_dt, ssm_conv_w[i], ssm_conv_b[i], ssm_dt_bias[i],
                         ssm_a_log[i], ssm_d[i], ssm_norm[i])
        ys = (y_a, y_b, y_c, y_d)
        gates = jax.nn.sigmoid(gates.astype(jnp.float32)).astype(x.dtype)
        gates = gates.reshape(Bsz, S, N_BRANCH, D)
        merged = sum(gates[:, :, j] * (ys[j] @ w_branch[i, j]) for j in range(N_BRANCH))
        x = x + merged @ w_out[i]
        h = _rmsnorm(x, norm_ffn[i])
        x = x + (jax.nn.silu(h @ w_ffn_gate[i]) * (h @ w_ffn_up[i])) @ w_ffn_down[i]
    return x
```

```python
import math
from contextlib import ExitStack
import numpy as np
import concourse.bass as bass
import concourse.mybir as mybir
from concourse.bass_utils import run_bass_kernel_spmd

F32 = mybir.dt.float32
BF16 = mybir.dt.bfloat16
ALU = mybir.AluOpType
AF = mybir.ActivationFunctionType

D = 4096
KC = D // 128
WM = 1024
EPS = 1e-6
NB_MIX = 86
NB_GATE = 128
NP = 256
SAME_ENG_SYNC = True


class Op:
    __slots__ = ("eng", "fn", "deps", "needed", "mark", "sem", "is_dma")

    def __init__(self, eng, fn, is_dma):
        self.eng = eng; self.fn = fn; self.deps = []; self.needed = False
        self.mark = None; self.sem = None; self.is_dma = is_dma


class Plan:
    ENGS = ("sync", "tensor", "vector", "scalar", "gpsimd")

    def __init__(self, nc, es):
        self.nc = nc; self.es = es
        self.ops = {e: [] for e in self.ENGS}
        self.last_w = {}; self.readers = {}
        self.esem = {e: es.enter_context(nc.semaphore("es_" + e)) for e in self.ENGS}
        self.dsem = {}; self.dcount = {}
        self.pend = {e: [] for e in self.ENGS}; self.dmas = []

    def barrier(self):
        deps = [self.ops[e][-1] for e in self.ENGS if self.ops[e]] + self.dmas
        self.dmas = []
        for e in self.ENGS:
            self.pend[e] = list(deps)

    def add(self, eng, fn, reads=(), writes=(), dma_key=None):
        op = Op(eng, fn, dma_key is not None)
        deps = list(self.pend[eng]); self.pend[eng] = []
        for k in reads:
            w = self.last_w.get(k)
            if w is not None: deps.append(w)
        for k in writes:
            w = self.last_w.get(k)
            if w is not None: deps.append(w)
            deps.extend(self.readers.get(k, ()))
        seen = set()
        for d in deps:
            if id(d) in seen or d is op: continue
            seen.add(id(d))
            if (not d.is_dma) and d.eng == eng and not op.is_dma and (eng == "tensor" or not SAME_ENG_SYNC):
                continue
            d.needed = True
            op.deps.append(d)
        for k in reads:
            self.readers.setdefault(k, []).append(op)
        for k in writes:
            self.last_w[k] = op; self.readers[k] = []
        if op.is_dma:
            if dma_key not in self.dsem:
                self.dsem[dma_key] = self.es.enter_context(self.nc.semaphore("ds%d" % len(self.dsem)))
                self.dcount[dma_key] = 0
            self.dcount[dma_key] += 16
            op.sem = self.dsem[dma_key]; op.mark = self.dcount[dma_key]; op.needed = True
            self.dmas.append(op)
        self.ops[eng].append(op)
        return op

    def emit(self, block, final_waits):
        for e in self.ENGS:
            c = 0
            for op in self.ops[e]:
                if not op.is_dma and op.needed:
                    c += 1; op.mark = c; op.sem = self.esem[e]
        plan = self

        def run(engname):
            def body(eng):
                waited = {}
                for op in plan.ops[engname]:
                    for d in op.deps:
                        key = id(d.sem)
                        if waited.get(key, 0) >= d.mark: continue
                        waited[key] = d.mark
                        eng.wait_ge(d.sem, d.mark)
                    ins = op.fn(eng)
                    if op.is_dma:
                        ins.then_inc(op.sem, 16)
                    elif op.needed:
                        ins.then_inc(op.sem, 1)
                if engname == "sync":
                    for d in final_waits:
                        eng.wait_ge(d.sem, d.mark)
            return body
        block.sync(run("sync")); block.tensor(run("tensor")); block.vector(run("vector"))
        block.scalar(run("scalar")); block.gpsimd(run("gpsimd"))


def in_col_perm():
    o = {}
    names = ["a_b", "a_c", "a_x", "b_u", "b_v", "c_q", "c_k", "c_v", "d_z"]
    off = 0
    for n in names:
        o[n] = off; off += WM
    o["d_xs"] = off; off += WM
    o["d_B"] = off; off += 256
    o["d_C"] = off; off += 256
    o["d_dt"] = off; off += 16
    o["gates"] = off
    cols = []
    for c in range(8):
        for n in ["a_b", "a_c", "a_x", "b_u", "b_v", "c_q", "c_k", "c_v", "d_z", "d_xs"]:
            cols.extend(range(o[n] + c * 128, o[n] + (c + 1) * 128))
    for g in range(2):
        cols.extend(range(o["d_B"] + g * 128, o["d_B"] + (g + 1) * 128))
        cols.extend(range(o["d_C"] + g * 128, o["d_C"] + (g + 1) * 128))
    cols.extend(range(o["d_dt"], o["d_dt"] + 16)); cols.extend([-1] * 112)
    cols.extend([-1] * 128)
    assert len(cols) == NB_MIX * 128
    cols.extend(range(o["gates"], o["gates"] + 4 * D))
    return np.array(cols, dtype=np.int64)


def panelize(w, npw):
    K, N = w.shape
    assert K % 128 == 0 and N % npw == 0
    a = w.reshape(K // 128, 128, N // npw, npw)
    return np.ascontiguousarray(a.transpose(2, 1, 0, 3)).reshape(N // npw, 128, (K // 128) * npw)


def vec_cols(v):
    return np.ascontiguousarray(v.reshape(-1, 128).T)


PO = {}
def _po():
    off = 0
    for name, n in [("norm_mix", 32), ("norm_ffn", 32), ("conv_a", 24), ("sgu_norm", 8), ("qn", 1), ("kn", 1),
                    ("cw_x", 32), ("cb_x", 8), ("cw_bc", 16), ("cb_bc", 4), ("ssm_norm", 8), ("dskip", 8)]:
        PO[name] = off; off += n
    PO["_n"] = off
_po()


def build_params(l, inp):
    p = np.zeros((128, PO["_n"]), np.float32)
    p[:, PO["norm_mix"]:PO["norm_mix"] + 32] = vec_cols(inp["norm_mix"][l])
    p[:, PO["norm_ffn"]:PO["norm_ffn"] + 32] = vec_cols(inp["norm_ffn"][l])
    ca = inp["conv_a"][l]
    for c in range(8):
        p[:, PO["conv_a"] + 3 * c:PO["conv_a"] + 3 * c + 3] = ca[:, c * 128:(c + 1) * 128].T
    p[:, PO["sgu_norm"]:PO["sgu_norm"] + 8] = vec_cols(inp["sgu_norm"][l])
    p[:, PO["qn"]] = inp["q_norm"][l]; p[:, PO["kn"]] = inp["k_norm"][l]
    cw = inp["ssm_conv_w"][l]; cb = inp["ssm_conv_b"][l]
    for c in range(8):
        p[:, PO["cw_x"] + 4 * c:PO["cw_x"] + 4 * c + 4] = cw[:, c * 128:(c + 1) * 128].T
        p[:, PO["cb_x"] + c] = cb[c * 128:(c + 1) * 128]
    for g in range(2):
        for j in range(2):
            o = 1024 + j * 256 + g * 128
            i = g * 2 + j
            p[:, PO["cw_bc"] + 4 * i:PO["cw_bc"] + 4 * i + 4] = cw[:, o:o + 128].T
            p[:, PO["cb_bc"] + i] = cb[o:o + 128]
    p[:, PO["ssm_norm"]:PO["ssm_norm"] + 8] = vec_cols(inp["ssm_norm"][l])
    p[:, PO["dskip"]:PO["dskip"] + 8] = np.repeat(inp["ssm_d"][l].reshape(8, 2), 64, axis=1).T
    return p


def small_inputs(l, inp):
    return {"par%d" % l: build_params(l, inp),
            "dtb%d" % l: np.concatenate([inp["ssm_dt_bias"][l], inp["ssm_a_log"][l], inp["ssm_d"][l]])[None, :].astype(np.float32),
            "sguw%d" % l: np.ascontiguousarray(inp["sgu_w"][l].transpose(0, 2, 1)),
            "sgub%d" % l: np.ascontiguousarray(inp["sgu_b"][l].reshape(1, 1024))}


def build_consts():
    s = np.arange(128)[:, None]
    t = np.arange(512)[None, :]
    c = np.zeros((128, 4 * 512 + 4 * 128), np.float32)
    for r in range(4):
        c[:, r * 512:(r + 1) * 512] = ((r * 128 + s) < t)
    t1 = np.arange(128)[None, :]
    o = 4 * 512
    c[:, o:o + 128] = (s <= t1)
    c[:, o + 128:o + 256] = -1.0 * (s >= t1)
    c[:, o + 256:o + 384] = -1.0
    c[:, o + 384:o + 512] = np.eye(128)
    return c


def build_nc(S, DFF, depth, stop_after=None, dbg=False):
    TB = 512
    NTB = S // TB
    KF = DFF // 128
    nc = bass.Bass("TRN2", target_bir_lowering=False)
    es = ExitStack()
    dr = lambda name, shape, dt, kind="Internal": nc.dram_tensor(name, shape, dt, kind=kind).ap()
    xin = dr("xT", [D, S], F32, "ExternalInput")
    yout = dr("yT", [D, S], F32, "ExternalOutput")
    consts_d = dr("consts", [128, 4 * 512 + 512], F32, "ExternalInput")
    NPI = (NB_MIX + NB_GATE) // 2
    W = []
    need = {"p1": ("w_in",), "p2": ("w_in",), "p3": ("w_in", "w_br", "w_out")}.get(stop_after, ("w_in", "w_br", "w_out", "w_g", "w_u", "w_d"))
    ikind = "ExternalOutput" if dbg else "Internal"
    for l in range(depth):
        W.append(dict(
            par=dr("par%d" % l, [128, PO["_n"]], F32, "ExternalInput"),
            dtb=dr("dtb%d" % l, [1, 48], F32, "ExternalInput"),
            sguw=dr("sguw%d" % l, [8, 128, 128], F32, "ExternalInput"),
            sgub=dr("sgub%d" % l, [1, 1024], F32, "ExternalInput"),
        ))
        wshapes = dict(w_in=[NPI, 128, KC * NP], w_br=[D // NP, 128, KC * NP], w_out=[D // NP, 128, KC * NP],
                       w_g=[DFF // NP, 128, KC * NP], w_u=[DFF // NP, 128, KC * NP], w_d=[D // 64, 128, KF * 64])
        for k in need:
            W[l][k] = dr("%s%d" % (k, l), wshapes[k], F32, "ExternalInput")
            W[l][k + "_b"] = dr("%s%d_b" % (k, l), wshapes[k], BF16)
    x1 = dr("x1", [D, S], F32)
    x2 = dr("x2", [D, S], F32)
    proj = dr("proj", [NB_MIX * 128, S], BF16, ikind)
    gates = dr("gates", [NB_GATE * 128, S], BF16, ikind)
    ymix = dr("ymix", [32 * 128, S], BF16, ikind)
    dbg_d = dr("dbg", [128, 2048], F32, "ExternalOutput") if dbg else None

    sb = lambda name, shape, dt: es.enter_context(nc.sbuf_tensor(name, shape, dt))
    ps = lambda name, shape, dt=F32: es.enter_context(nc.psum_tensor(name, shape, dt))
    P = Plan(nc, es)

    cb16 = sb("consts_b", [128, 4 * 512 + 512], BF16)
    cf32 = sb("consts_f", [128, 512], F32)
    ones_f = sb("ones_f", [128, 128], F32)
    ones_b = sb("ones_b", [128, 128], BF16)
    par = sb("par_s", [128, PO["_n"]], F32)
    CO = 4 * 512
    maskLT = lambda r: cb16[:, r * 512:(r + 1) * 512]
    LE_f = cf32[:, 0:128]
    LE_b = cb16[:, CO:CO + 128]
    NU_b = cb16[:, CO + 128:CO + 256]
    NEG1_b = cb16[:, CO + 256:CO + 384]
    ident_b = cb16[:, CO + 384:CO + 512]
    ident_f = cf32[:, 384:512]

    HT = sb("HT", [128, KC, TB], BF16)
    WPN = KC * NP
    WP = [sb("WP%d" % i, [128, WPN], BF16) for i in range(3)]
    BIGF = max(KF, 56) * TB // 2
    BIG = sb("BIG", [128, BIGF], F32)
    BIGb = BIG[:, :].bitcast(BF16)
    STG = [sb("STG%d" % i, [128, TB], BF16) for i in range(4)]
    GT = [sb("GT%d" % i, [128, TB], BF16) for i in range(6)]
    STF = [sb("STF%d" % i, [128, TB], F32) for i in range(5)]
    RS = sb("RS", [128, TB], F32)
    PSB = [ps("PS%d" % i, [128, TB]) for i in range(7)]
    PT = ps("PT", [128, 1024], BF16)
    soff = [0]

    def sf(n):
        a = BIG[:, soff[0]:soff[0] + n]; soff[0] += n; assert soff[0] <= BIGF; return a

    def sh(n):
        m = (n + 1) // 2
        a = BIG[:, soff[0]:soff[0] + m].bitcast(BF16); soff[0] += m; assert soff[0] <= BIGF; return a

    cnt = {"wp": 0, "ps": 0, "stg": 0, "stf": 0, "gt": 0}

    def rot(name, n):
        i = cnt[name] % n; cnt[name] += 1; return i

    for q in range(5):
        P.add("sync", (lambda e, q=q: e.dma_start(out=STF[q][:], in_=consts_d[:, q * 512:(q + 1) * 512])), writes=[("stf", q)], dma_key=("stfl", q))
        P.add("vector", (lambda e, q=q: e.tensor_copy(out=cb16[:, q * 512:(q + 1) * 512], in_=STF[q][:])), reads=[("stf", q)], writes=["cb16"])
    P.add("vector", lambda e: e.tensor_copy(out=cf32[:], in_=STF[4][:]), reads=[("stf", 4)], writes=["cf32"])
    P.add("vector", lambda e: e.memset(ones_f[:], 1.0), writes=["ones_f"])
    P.add("vector", lambda e: e.memset(ones_b[:], 1.0), writes=["ones_b"])

    def cast_weights(l, names):
        for k in names:
            src = W[l][k]; dst = W[l][k + "_b"]
            for pi in range(src.shape[0]):
                P.add("gpsimd", (lambda e, s=src, d=dst, pi=pi: e.dma_start(out=d[pi], in_=s[pi])),
                      writes=[(k + "_b", l, pi)], dma_key=("cast", pi % 8))

    def gemm(wname, l, act, act_key, kc_n, npw, n_panels, epilogue, panel_ok=None):
        wb = W[l][wname + "_b"]
        per = max(npw // 128, 1)
        for pi in range(n_panels):
            if panel_ok is not None and not panel_ok(pi): continue
            slot = rot("wp", 3)
            wp = WP[slot]
            P.add("sync", (lambda e, wp=wp, pi=pi: e.dma_start(out=wp[:, 0:kc_n * npw], in_=wb[pi])),
                  reads=[(wname + "_b", l, pi)], writes=[("wp", slot)], dma_key=("wp", slot))
            for j in range(per):
                b = rot("ps", 4)
                pt = PSB[b]
                for kc in range(kc_n):
                    P.add("tensor", (lambda e, pt=pt, wp=wp, kc=kc, j=j: e.matmul(
                        pt[0:min(npw, 128), :], lhsT=wp[:, kc * npw + j * 128: kc * npw + j * 128 + min(npw, 128)], rhs=act(kc),
                        start=(kc == 0), stop=(kc == kc_n - 1))),
                        reads=[("wp", slot), act_key(kc) if callable(act_key) else act_key], writes=[("ps", b)])
                epilogue(pi * per + j, pt, ("ps", b))

    def store(eng_q, dst_ap, src_tile, src_key, dst_key):
        return P.add(eng_q, (lambda e: e.dma_start(out=dst_ap, in_=src_tile)), reads=[src_key], writes=[dst_key],
                     dma_key=("st", src_key))

    def rmsnorm_block(src, tb, gcol, xkey=None):
        t0 = tb * TB
        pt = PSB[4]
        for kc in range(KC):
            i = rot("stf", 5)
            P.add("sync", (lambda e, i=i, kc=kc: e.dma_start(out=STF[i][:], in_=src[kc * 128:(kc + 1) * 128, t0:t0 + TB])),
                  reads=([xkey(kc)] if xkey else []), writes=[("stf", i)], dma_key=("stfl", i))
            P.add("scalar", (lambda e, i=i: e.activation(out=STF[i][:], in_=STF[i][:], func=AF.Square)),
                  reads=[("stf", i)], writes=[("stf", i)])
            P.add("tensor", (lambda e, i=i, kc=kc: e.matmul(pt[:], lhsT=ones_f[:], rhs=STF[i][:], start=(kc == 0), stop=(kc == KC - 1))),
                  reads=[("stf", i), "ones_f"], writes=[("ps", 4)])
        P.add("vector", lambda e: e.tensor_scalar(out=RS[:], in0=pt[:], scalar1=1.0 / D, scalar2=EPS, op0=ALU.mult, op1=ALU.add),
              reads=[("ps", 4)], writes=["RS"])
        P.add("scalar", lambda e: e.activation(out=RS[:], in_=RS[:], func=AF.Sqrt), reads=["RS"], writes=["RS"])
        P.add("vector", lambda e: e.reciprocal(out=RS[:], in_=RS[:]), reads=["RS"], writes=["RS"])
        for kc in range(KC):
            i = rot("stf", 5)
            P.add("sync", (lambda e, i=i, kc=kc: e.dma_start(out=STF[i][:], in_=src[kc * 128:(kc + 1) * 128, t0:t0 + TB])),
                  reads=([xkey(kc)] if xkey else []), writes=[("stf", i)], dma_key=("stfl", i))
            P.add("vector", (lambda e, kc=kc, i=i: e.scalar_tensor_tensor(
                out=HT[:, kc, :], in0=STF[i][:], scalar=par[:, gcol + kc:gcol + kc + 1], in1=RS[:], op0=ALU.mult, op1=ALU.mult)),
                reads=[("stf", i), "RS", "par"], writes=["HT"])

    SP = TB
    NSP = S // SP

    def ld(dst, dkey, src_ap, skey, q="sync"):
        return P.add(q, (lambda e: e.dma_start(out=dst, in_=src_ap)), reads=[skey], writes=[dkey], dma_key=("ld", dkey))

    def mix_c(l, Wl):
        soff[0] = 0
        QN = sh(S); KN = sh(S); VT = sh(S)
        LQ = sh(SP); SQ = sf(SP); RQ = sf(SP)
        E = [sf(SP) for _ in range(2)]
        SPT = [sh(SP) for _ in range(2)]
        SPS = sh(SP)
        ATT = [sh(SP) for _ in range(2)]
        YC = sh(SP)
        GQ = sf(2)
        scale = 1.0 / math.sqrt(128.0)
        P.add("vector", lambda e: e.tensor_scalar(out=GQ[:, 0:1], in0=par[:, PO["qn"]:PO["qn"] + 1], scalar1=scale, scalar2=None, op0=ALU.mult),
              reads=["par"], writes=["GQ"])
        P.add("vector", lambda e: e.tensor_copy(out=GQ[:, 1:2], in_=par[:, PO["kn"]:PO["kn"] + 1]), reads=["par"], writes=["GQ"])
        for c in range(8):
            for i in range(NSP):
                t0 = i * SP
                for (j, dstt, dk, gi) in ((5, QN, "QN", 0), (6, KN, "KN", 1)):
                    nb = c * 10 + j
                    ld(LQ, "LQ", proj[nb * 128:(nb + 1) * 128, t0:t0 + SP], ("proj", nb, t0))
                    P.add("vector", lambda e: e.tensor_tensor(out=SQ, in0=LQ, in1=LQ, op=ALU.mult), reads=["LQ"], writes=["SQ"])
                    P.add("tensor", lambda e: e.matmul(PSB[4][:], lhsT=ones_f[:], rhs=SQ, start=True, stop=True), reads=["SQ", "ones_f"], writes=[("ps", 4)])
                    P.add("vector", lambda e: e.tensor_scalar(out=RQ, in0=PSB[4][:], scalar1=1.0 / 128, scalar2=EPS, op0=ALU.mult, op1=ALU.add),
                          reads=[("ps", 4)], writes=["RQ"])
                    P.add("scalar", lambda e: e.activation(out=RQ, in_=RQ, func=AF.Sqrt), reads=["RQ"], writes=["RQ"])
                    P.add("vector", lambda e: e.reciprocal(out=RQ, in_=RQ), reads=["RQ"], writes=["RQ"])
                    P.add("vector", (lambda e, dstt=dstt, gi=gi, t0=t0: e.scalar_tensor_tensor(out=dstt[:, t0:t0 + SP], in0=LQ, scalar=GQ[:, gi:gi + 1], in1=RQ,
                                                                                              op0=ALU.mult, op1=ALU.mult)),
                          reads=["LQ", "RQ", "GQ"], writes=[dk])
                nb = c * 10 + 7
                ld(LQ, "LQ", proj[nb * 128:(nb + 1) * 128, t0:t0 + SP], ("proj", nb, t0))
                for j in range(SP // 128):
                    P.add("tensor", (lambda e, j=j: e.transpose(PT[:, j * 128:(j + 1) * 128], LQ[:, j * 128:(j + 1) * 128], ident_b)),
                          reads=["LQ", "cb16"], writes=["PT"])
                P.add("vector", (lambda e, t0=t0: e.tensor_copy(out=VT[:, t0:t0 + SP], in_=PT[:, 0:SP])), reads=["PT"], writes=["VT"])
            for i in range(NSP):
                t0 = i * SP
                nq = SP // 128
                kbs = list(range(i * nq + nq - 1, -1, -1))
                po = PSB[6]
                for n, kb in enumerate(kbs):
                    r = kb - i * nq
                    a = n % 2
                    pa = PSB[a]; pb = PSB[2 + a]
                    P.add("tensor", (lambda e, pa=pa, kb=kb, t0=t0: e.matmul(pa[:], lhsT=KN[:, kb * 128:(kb + 1) * 128], rhs=QN[:, t0:t0 + SP], start=True, stop=True)),
                          reads=["KN", "QN"], writes=[("ps", a)])
                    P.add("scalar", (lambda e, pa=pa, a=a: e.activation(out=E[a], in_=pa[:], func=AF.Exp)), reads=[("ps", a)], writes=[("E", a)])
                    P.add("scalar", (lambda e, a=a: e.activation(out=SPT[a], in_=E[a], func=AF.Ln, bias=1.0)), reads=[("E", a)], writes=[("SPT", a)])
                    if r >= 0:
                        P.add("gpsimd", (lambda e, a=a, r=r: e.tensor_tensor(out=SPT[a], in0=SPT[a], in1=maskLT(r), op=ALU.mult)),
                              reads=[("SPT", a), "cb16"], writes=[("SPT", a)])
                    P.add("tensor", (lambda e, pb=pb, kb=kb, t0=t0: e.matmul(pb[:], lhsT=KN[:, kb * 128:(kb + 1) * 128], rhs=QN[:, t0:t0 + SP], start=True, stop=False)),
                          reads=["KN", "QN"], writes=[("ps", 2 + a)])
                    P.add("tensor", (lambda e, pb=pb, a=a, n=n: e.matmul(pb[:], lhsT=NU_b, rhs=SPT[a], start=False, stop=(n == 0))),
                          reads=[("SPT", a), "cb16"], writes=[("ps", 2 + a)])
                    if n > 0:
                        P.add("tensor", (lambda e, pb=pb: e.matmul(pb[:], lhsT=NEG1_b, rhs=SPS, start=False, stop=True)),
                              reads=["SPS", "cb16"], writes=[("ps", 2 + a)])
                    P.add("scalar", (lambda e, pb=pb, a=a: e.activation(out=ATT[a], in_=pb[:], func=AF.Exp)), reads=[("ps", 2 + a)], writes=[("ATT", a)])
                    if r >= 0:
                        P.add("gpsimd", (lambda e, a=a, r=r: e.tensor_tensor(out=ATT[a], in0=ATT[a], in1=maskLT(r), op=ALU.mult)),
                              reads=[("ATT", a), "cb16"], writes=[("ATT", a)])
                    P.add("tensor", (lambda e, kb=kb, a=a, n=n: e.matmul(po[:], lhsT=VT[:, kb * 128:(kb + 1) * 128], rhs=ATT[a], start=(n == 0), stop=(n == len(kbs) - 1))),
                          reads=["VT", ("ATT", a)], writes=[("ps", 6)])
                    if n == 0:
                        P.add("gpsimd", (lambda e, a=a: e.tensor_copy(out=SPS, in_=SPT[a])), reads=[("SPT", a)], writes=["SPS"])
                    elif n < len(kbs) - 1:
                        P.add("gpsimd", (lambda e, a=a: e.tensor_tensor(out=SPS, in0=SPS, in1=SPT[a], op=ALU.add)), reads=[("SPT", a), "SPS"], writes=["SPS"])
                P.add("vector", lambda e: e.tensor_copy(out=YC, in_=po[:]), reads=[("ps", 6)], writes=["YC"])
                store("sync", ymix[(16 + c) * 128:(16 + c + 1) * 128, t0:t0 + SP], YC, "YC", ("ymix", 16 + c, t0))

    def mix_d(l, Wl):
        soff[0] = 0
        LXS = sh(SP); LBm = sh(SP); LCm = sh(SP); LDT = sh(SP)
        XE = sf(SP + 3); BE = sf(SP + 3); CE = sf(SP + 3)
        HX = sf(3); HB = sf(3); HC = sf(3)
        XC = sh(SP); BC = sh(SP); CC = sh(SP); ACCD = sf(SP); YD = sh(SP)
        DTB = sf(48); DTR = sf(48); A2 = sf(16)
        DT2 = sf(2); DA = sf(2); CUMC = sf(2); CL = sf(2); DE = sf(2); CD = sf(2); TE = sf(2)
        DAB = [sf(128) for _ in range(2)]
        TD = [sf(128) for _ in range(2)]
        MH = [sh(128) for _ in range(2)]
        ECB = [sf(128) for _ in range(2)]
        CS = [sh(128) for _ in range(2)]
        XDTA = sh(128); XDTB = sh(128); XDE = sh(128); BT = sh(128)
        HF = sf(128); HTA = sh(128); HTB = sh(128)
        DBG = sf(2048) if dbg else None
        P.add("sync", lambda e: e.dma_start(out=DTR[0:1, :], in_=Wl["dtb"][:, :]), writes=["DTR"], dma_key="dtr")
        P.add("tensor", lambda e: e.matmul(PSB[0][:, 0:48], lhsT=ones_f[0:1, :], rhs=DTR[0:1, :], start=True, stop=True), reads=["DTR", "ones_f"], writes=[("ps", 0)])
        P.add("vector", lambda e: e.tensor_copy(out=DTB, in_=PSB[0][:, 0:48]), reads=[("ps", 0)], writes=["DTB"])
        P.add("scalar", lambda e: e.activation(out=A2, in_=DTB[:, 16:32], func=AF.Exp), reads=["DTB"], writes=["A2"])
        P.add("vector", lambda e: e.tensor_scalar(out=A2, in0=A2, scalar1=-1.0, scalar2=None, op0=ALU.mult), reads=["A2"], writes=["A2"])
        P.add("vector", lambda e: e.memset(XDTA, 0.0), writes=["XDTA"])
        P.add("vector", lambda e: e.memset(XDTB, 0.0), writes=["XDTB"])
        for c in range(8):
            g = c // 4
            h0 = 2 * c
            P.add("vector", lambda e: e.memset(HF, 0.0), writes=["HF"])
            P.add("vector", lambda e: e.memset(HTA, 0.0), writes=["HTA"])
            P.add("vector", lambda e: e.memset(HTB, 0.0), writes=["HTB"])
            for i in range(NSP):
                t0 = i * SP
                srcs = ((LXS, "LXS", c * 10 + 9, XE, "XE", HX, "HX", XC, "XC", PO["cw_x"] + 4 * c, PO["cb_x"] + c),
                        (LBm, "LBm", 80 + 2 * g, BE, "BE", HB, "HB", BC, "BC", PO["cw_bc"] + 4 * (2 * g), PO["cb_bc"] + 2 * g),
                        (LCm, "LCm", 81 + 2 * g, CE, "CE", HC, "HC", CC, "CC", PO["cw_bc"] + 4 * (2 * g + 1), PO["cb_bc"] + 2 * g + 1))
                for (LT, lk, nb, EXT, ek, HL, hk, OUT, ok, wc, bc) in srcs:
                    ld(LT, lk, proj[nb * 128:(nb + 1) * 128, t0:t0 + SP], ("proj", nb, t0))
                    if i == 0:
                        P.add("vector", (lambda e, EXT=EXT: e.memset(EXT[:, 0:3], 0.0)), writes=[ek])
                    else:
                        P.add("vector", (lambda e, EXT=EXT, HL=HL: e.tensor_copy(out=EXT[:, 0:3], in_=HL)), reads=[hk], writes=[ek])
                    P.add("vector", (lambda e, EXT=EXT, LT=LT: e.tensor_copy(out=EXT[:, 3:3 + SP], in_=LT)), reads=[lk], writes=[ek])
                    P.add("vector", (lambda e, EXT=EXT, wc=wc: e.tensor_scalar(out=ACCD, in0=EXT[:, 0:SP], scalar1=par[:, wc:wc + 1], scalar2=None, op0=ALU.mult)),
                          reads=[ek, "par"], writes=["ACCD"])
                    for k in (1, 2, 3):
                        P.add("vector", (lambda e, EXT=EXT, wc=wc, k=k: e.scalar_tensor_tensor(out=ACCD, in0=EXT[:, k:k + SP], scalar=par[:, wc + k:wc + k + 1],
                                                                                               in1=ACCD, op0=ALU.mult, op1=ALU.add)),
                              reads=[ek, "par", "ACCD"], writes=["ACCD"])
                    P.add("vector", (lambda e, EXT=EXT, HL=HL: e.tensor_copy(out=HL, in_=EXT[:, SP:SP + 3])), reads=[ek], writes=[hk])
                    P.add("scalar", (lambda e, OUT=OUT, bc=bc: e.activation(out=OUT, in_=ACCD, func=AF.Silu, bias=par[:, bc:bc + 1])),
                          reads=["ACCD", "par"], writes=[ok])
                ld(LDT, "LDT", proj[84 * 128:85 * 128, t0:t0 + SP], ("proj", 84, t0))
                for j in range(SP // 128):
                    cs = slice(j * 128, (j + 1) * 128)
                    P.add("tensor", (lambda e, cs=cs: e.transpose(PT[:, 0:128], LDT[:, cs], ident_b)), reads=["LDT", "cb16"], writes=[("PT", 0)])
                    P.add("vector", (lambda e, h0=h0: e.tensor_tensor(out=DT2, in0=PT[:, h0:h0 + 2], in1=DTB[:, h0:h0 + 2], op=ALU.add)),
                          reads=[("PT", 0), "DTB"], writes=["DT2"])
                    P.add("scalar", lambda e: e.activation(out=DT2, in_=DT2, func=AF.Exp), reads=["DT2"], writes=["DT2"])
                    P.add("scalar", lambda e: e.activation(out=DT2, in_=DT2, func=AF.Ln, bias=1.0), reads=["DT2"], writes=["DT2"])
                    P.add("vector", (lambda e, h0=h0: e.tensor_tensor(out=DA, in0=DT2, in1=A2[:, h0:h0 + 2], op=ALU.mult)), reads=["DT2", "A2"], writes=["DA"])
                    for h in range(2):
                        P.add("vector", (lambda e, h=h: e.tensor_scalar(out=DAB[h], in0=ones_f[:], scalar1=DA[:, h:h + 1], scalar2=None, op0=ALU.mult)),
                              reads=["DA", "ones_f"], writes=[("DAB", h)])
                        P.add("tensor", (lambda e, h=h: e.matmul(PSB[h][:, 0:128], lhsT=DAB[h], rhs=LE_f, start=True, stop=True)),
                              reads=[("DAB", h), "cf32"], writes=[("ps", h)])
                    P.add("tensor", lambda e: e.matmul(PSB[2][:, 0:2], lhsT=LE_f, rhs=DA, start=True, stop=True), reads=["DA", "cf32"], writes=[("ps", 2)])
                    P.add("vector", lambda e: e.tensor_copy(out=CUMC, in_=PSB[2][:, 0:2]), reads=[("ps", 2)], writes=["CUMC"])
                    for h in range(2):
                        P.add("vector", (lambda e, h=h: e.tensor_copy(out=CL[:, h:h + 1], in_=PSB[h][:, 127:128])), reads=[("ps", h)], writes=["CL"])
                    P.add("tensor", (lambda e, cs=cs: e.matmul(PSB[3][:, 0:128], lhsT=BC[:, cs], rhs=CC[:, cs], start=True, stop=True)),
                          reads=["BC", "CC"], writes=[("ps", 3)])
                    P.add("tensor", (lambda e, cs=cs: e.transpose(PT[:, 128:256], XC[:, cs], ident_b)), reads=["XC", "cb16"], writes=[("PT", 1)])
                    P.add("vector", lambda e: e.tensor_scalar(out=XDTA[:, 0:64], in0=PT[:, 128:192], scalar1=DT2[:, 0:1], scalar2=None, op0=ALU.mult),
                          reads=[("PT", 1), "DT2"], writes=["XDTA"])
                    P.add("vector", lambda e: e.tensor_scalar(out=XDTB[:, 64:128], in0=PT[:, 192:256], scalar1=DT2[:, 1:2], scalar2=None, op0=ALU.mult),
                          reads=[("PT", 1), "DT2"], writes=["XDTB"])
                    for h in range(2):
                        P.add("vector", (lambda e, h=h: e.tensor_scalar(out=TD[h], in0=PSB[h][:, 0:128], scalar1=CUMC[:, h:h + 1], scalar2=0.0,
                                                                        op0=ALU.subtract, op1=ALU.min)),
                              reads=[("ps", h), "CUMC"], writes=[("TD", h)])
                        P.add("scalar", (lambda e, h=h: e.activation(out=TD[h], in_=TD[h], func=AF.Exp)), reads=[("TD", h)], writes=[("TD", h)])
                        P.add("vector", (lambda e, h=h: e.tensor_tensor(out=TD[h], in0=TD[h], in1=LE_f, op=ALU.mult)), reads=[("TD", h), "cf32"], writes=[("TD", h)])
                        P.add("vector", (lambda e, h=h: e.tensor_tensor(out=MH[h], in0=PSB[3][:, 0:128], in1=TD[h], op=ALU.mult)),
                              reads=[("ps", 3), ("TD", h)], writes=[("MH", h)])
                        P.add("scalar", (lambda e, h=h: e.activation(out=ECB[h], in_=PSB[h][:, 0:128], func=AF.Exp)), reads=[("ps", h)], writes=[("ECB", h)])
                        P.add("vector", (lambda e, h=h, cs=cs: e.tensor_tensor(out=CS[h], in0=CC[:, cs], in1=ECB[h], op=ALU.mult)),
                              reads=["CC", ("ECB", h)], writes=[("CS", h)])
                    P.add("tensor", lambda e: e.matmul(PSB[4][:, 0:128], lhsT=XDTA, rhs=MH[0], start=True, stop=False), reads=["XDTA", ("MH", 0)], writes=[("ps", 4)])
                    P.add("tensor", lambda e: e.matmul(PSB[4][:, 0:128], lhsT=XDTB, rhs=MH[1], start=False, stop=False), reads=["XDTB", ("MH", 1)], writes=[("ps", 4)])
                    P.add("tensor", lambda e: e.matmul(PSB[4][:, 0:128], lhsT=HTA, rhs=CS[0], start=False, stop=False), reads=["HTA", ("CS", 0)], writes=[("ps", 4)])
                    P.add("tensor", lambda e: e.matmul(PSB[4][:, 0:128], lhsT=HTB, rhs=CS[1], start=False, stop=True), reads=["HTB", ("CS", 1)], writes=[("ps", 4)])
                    P.add("vector", (lambda e, cs=cs, c=c: e.scalar_tensor_tensor(out=YD[:, cs], in0=XC[:, cs], scalar=par[:, PO["dskip"] + c:PO["dskip"] + c + 1],
                                                                                   in1=PSB[4][:, 0:128], op0=ALU.mult, op1=ALU.add)),
                          reads=["XC", "par", ("ps", 4)], writes=["YD"])
                    for h in range(2):
                        P.add("scalar", (lambda e, h=h: e.activation(out=DE[:, h:h + 1], in_=CUMC[:, h:h + 1], func=AF.Exp, scale=-1.0, bias=CL[:, h:h + 1])),
                              reads=["CUMC", "CL"], writes=["DE"])
                    P.add("scalar", lambda e: e.activation(out=CD, in_=CL, func=AF.Exp), reads=["CL"], writes=["CD"])
                    P.add("vector", lambda e: e.tensor_scalar(out=XDE[:, 0:64], in0=XDTA[:, 0:64], scalar1=DE[:, 0:1], scalar2=None, op0=ALU.mult),
                          reads=["XDTA", "DE"], writes=["XDE"])
                    P.add("vector", lambda e: e.tensor_scalar(out=XDE[:, 64:128], in0=XDTB[:, 64:128], scalar1=DE[:, 1:2], scalar2=None, op0=ALU.mult),
                          reads=["XDTB", "DE"], writes=["XDE"])
                    P.add("tensor", (lambda e, cs=cs: e.transpose(PT[:, 256:384], BC[:, cs], ident_b)), reads=["BC", "cb16"], writes=[("PT", 2)])
                    P.add("vector", lambda e: e.tensor_copy(out=BT, in_=PT[:, 256:384]), reads=[("PT", 2)], writes=["BT"])
                    P.add("tensor", lambda e: e.matmul(PSB[5][:, 0:128], lhsT=BT, rhs=XDE, start=True, stop=True), reads=["BT", "XDE"], writes=[("ps", 5)])
                    P.add("vector", lambda e: e.scalar_tensor_tensor(out=HF[:, 0:64], in0=HF[:, 0:64], scalar=CD[:, 0:1], in1=PSB[5][:, 0:64], op0=ALU.mult, op1=ALU.add),
                          reads=["HF", "CD", ("ps", 5)], writes=["HF"])
                    P.add("vector", lambda e: e.scalar_tensor_tensor(out=HF[:, 64:128], in0=HF[:, 64:128], scalar=CD[:, 1:2], in1=PSB[5][:, 64:128], op0=ALU.mult, op1=ALU.add),
                          reads=["HF", "CD", ("ps", 5)], writes=["HF"])
                    P.add("vector", lambda e: e.tensor_copy(out=HTA[:, 0:64], in_=HF[:, 0:64]), reads=["HF"], writes=["HTA"])
                    P.add("vector", lambda e: e.tensor_copy(out=HTB[:, 64:128], in_=HF[:, 64:128]), reads=["HF"], writes=["HTB"])
                    if dbg and c == 0 and i == 0 and j == 0:
                        items = [(DTB, 48, "DTB"), (A2, 16, "A2"), (DT2, 2, "DT2"), (DA, 2, "DA"), (CUMC, 2, "CUMC"), (CL, 2, "CL"), (DE, 2, "DE"), (CD, 2, "CD"),
                                 (TD[0], 128, ("TD", 0)), (MH[0], 128, ("MH", 0)), (ECB[0], 128, ("ECB", 0)), (CS[0], 128, ("CS", 0)),
                                 (XDTA, 128, "XDTA"), (XC[:, 0:128], 128, "XC"), (BC[:, 0:128], 128, "BC"), (CC[:, 0:128], 128, "CC"), (HF, 128, "HF"),
                                 (YD[:, 0:128], 128, "YD"), (XDE, 128, "XDE"), (BT, 128, "BT")]
                        o = 0
                        for (ap_, w_, k_) in items:
                            P.add("vector", (lambda e, ap_=ap_, o=o, w_=w_: e.tensor_copy(out=DBG[:, o:o + w_], in_=ap_)), reads=[k_], writes=["DBG"])
                            print("DBGITEM", k_, o, w_)
                            o += w_
                        P.add("sync", lambda e: e.dma_start(out=dbg_d[:, :], in_=DBG), reads=["DBG"], writes=["dbg_d"], dma_key="dbg")
                store("sync", ymix[(24 + c) * 128:(25 + c) * 128, t0:t0 + SP], YD, "YD", ("ymix", 24 + c, t0))

    def phase3(l, Wl, src, dst):
        soff[0] = 0
        cast_weights(l, ["w_br", "w_out", "w_g", "w_u", "w_d"])
        YG = [sf(TB) for _ in range(4)]
        ZT = sh(TB); YT = sh(TB); T1 = sf(TB); T2 = sf(TB); R3 = sf(TB)
        macc = sf(TB)
        assert soff[0] * 2 <= 20 * TB
        for tb in range(NTB):
            t0 = tb * TB
            for m in range(3):
                for c in range(8):
                    kk = m * 8 + c
                    P.add("sync", (lambda e, kk=kk, t0=t0: e.dma_start(out=HT[:, kk, :], in_=ymix[kk * 128:(kk + 1) * 128, t0:t0 + TB])),
                          reads=[("ymix", kk, t0)], writes=[("HT", kk)], dma_key=("ldht", kk % 4))
            for g in range(2):
                for q in range(4):
                    c = g * 4 + q
                    ld(YT, "YT", ymix[(24 + c) * 128:(25 + c) * 128, t0:t0 + TB], ("ymix", 24 + c, t0))
                    nb = c * 10 + 8
                    ld(ZT, "ZT", proj[nb * 128:(nb + 1) * 128, t0:t0 + TB], ("proj", nb, t0))
                    P.add("scalar", lambda e: e.activation(out=T1, in_=ZT, func=AF.Silu), reads=["ZT"], writes=["T1"])
                    P.add("vector", (lambda e, q=q: e.tensor_tensor(out=YG[q], in0=YT, in1=T1, op=ALU.mult)), reads=["YT", "T1"], writes=[("YG", q)])
                    P.add("scalar", (lambda e, q=q: e.activation(out=T2, in_=YG[q], func=AF.Square)), reads=[("YG", q)], writes=["T2"])
                    P.add("tensor", (lambda e, q=q: e.matmul(PSB[4][:], lhsT=ones_f[:], rhs=T2, start=(q == 0), stop=(q == 3))),
                          reads=["T2", "ones_f"], writes=[("ps", 4)])
                P.add("vector", lambda e: e.tensor_scalar(out=R3, in0=PSB[4][:], scalar1=1.0 / 512, scalar2=EPS, op0=ALU.mult, op1=ALU.add),
                      reads=[("ps", 4)], writes=["R3"])
                P.add("scalar", lambda e: e.activation(out=R3, in_=R3, func=AF.Sqrt), reads=["R3"], writes=["R3"])
                P.add("vector", lambda e: e.reciprocal(out=R3, in_=R3), reads=["R3"], writes=["R3"])
                for q in range(4):
                    c = g * 4 + q
                    P.add("vector", (lambda e, q=q, c=c: e.scalar_tensor_tensor(out=HT[:, 24 + c, :], in0=YG[q], scalar=par[:, PO["ssm_norm"] + c:PO["ssm_norm"] + c + 1],
                                                                                 in1=R3, op0=ALU.mult, op1=ALU.mult)),
                          reads=[("YG", q), "R3", "par"], writes=[("HT", 24 + c)])
            wb = Wl["w_br_b"]
            for pi in range(D // NP):
                slot = rot("wp", 3); wp = WP[slot]
                P.add("sync", (lambda e, wp=wp, pi=pi: e.dma_start(out=wp[:, 0:KC * NP], in_=wb[pi])),
                      reads=[("w_br_b", l, pi)], writes=[("wp", slot)], dma_key=("wp", slot))
                for j in range(NP // 128):
                    nb = pi * (NP // 128) + j
                    for m in range(4):
                        b = rot("ps", 4); pt = PSB[b]
                        for c in range(8):
                            kk = m * 8 + c
                            P.add("tensor", (lambda e, pt=pt, wp=wp, kk=kk, j=j, c=c: e.matmul(
                                pt[:], lhsT=wp[:, kk * NP + j * 128:kk * NP + (j + 1) * 128], rhs=HT[:, kk, :], start=(c == 0), stop=(c == 7))),
                                reads=[("wp", slot), ("HT", kk)], writes=[("ps", b)])
                        gi = rot("gt", 6)
                        gr = m * 32 + nb
                        P.add("sync", (lambda e, gi=gi, gr=gr, t0=t0: e.dma_start(out=GT[gi][:], in_=gates[gr * 128:(gr + 1) * 128, t0:t0 + TB])),
                              reads=[("gates", gr, t0)], writes=[("gt", gi)], dma_key=("gt", gi))
                        if m == 0:
                            P.add("vector", (lambda e, pt=pt, gi=gi: e.tensor_tensor(out=macc, in0=pt[:], in1=GT[gi][:], op=ALU.mult)),
                                  reads=[("ps", b), ("gt", gi)], writes=["macc"])
                        else:
                            P.add("vector", (lambda e, pt=pt, gi=gi: e.tensor_tensor(out=T1, in0=pt[:], in1=GT[gi][:], op=ALU.mult)),
                                  reads=[("ps", b), ("gt", gi)], writes=["T1"])
                            if m < 3:
                                P.add("vector", lambda e: e.tensor_tensor(out=macc, in0=macc, in1=T1, op=ALU.add), reads=["macc", "T1"], writes=["macc"])
                            else:
                                P.add("vector", (lambda e, nb=nb: e.tensor_tensor(out=BIGb[:, (20 + nb) * TB:(20 + nb + 1) * TB], in0=macc, in1=T1, op=ALU.add)),
                                      reads=["macc", "T1"], writes=[("BIG", 20 + nb)])

            def epi_o(nb, pt, pkey, t0=t0):
                i = rot("stf", 5)
                P.add("sync", (lambda e: e.dma_start(out=STF[i][:], in_=src[nb * 128:(nb + 1) * 128, t0:t0 + TB])), writes=[("stf", i)], dma_key=("stfl", i))
                P.add("vector", (lambda e: e.tensor_tensor(out=STF[i][:], in0=pt[:], in1=STF[i][:], op=ALU.add)), reads=[pkey, ("stf", i)], writes=[("stf", i)])
                store("sync", x1[nb * 128:(nb + 1) * 128, t0:t0 + TB], STF[i][:], ("stf", i), ("x1", nb, t0))
            gemm("w_out", l, (lambda kc: BIGb[:, (20 + kc) * TB:(20 + kc + 1) * TB]), (lambda kc: ("BIG", 20 + kc)), KC, NP, D // NP, epi_o)
            if stop_after == "p3":
                P.barrier()
                continue
            rmsnorm_block(x1, tb, PO["norm_ffn"], xkey=lambda kc, t0=t0: ("x1", kc, t0))
            wg = Wl["w_g_b"]; wu = Wl["w_u_b"]
            for pi in range(DFF // NP):
                sg = rot("wp", 3); su = rot("wp", 3)
                P.add("sync", (lambda e, sg=sg, pi=pi: e.dma_start(out=WP[sg][:, 0:KC * NP], in_=wg[pi])), reads=[("w_g_b", l, pi)], writes=[("wp", sg)], dma_key=("wp", sg))
                P.add("sync", (lambda e, su=su, pi=pi: e.dma_start(out=WP[su][:, 0:KC * NP], in_=wu[pi])), reads=[("w_u_b", l, pi)], writes=[("wp", su)], dma_key=("wp", su))
                for j in range(NP // 128):
                    hb = pi * (NP // 128) + j
                    bg = rot("ps", 4); bu = rot("ps", 4)
                    for (slot, b) in ((sg, bg), (su, bu)):
                        for kc in range(KC):
                            P.add("tensor", (lambda e, slot=slot, b=b, kc=kc, j=j: e.matmul(
                                PSB[b][:], lhsT=WP[slot][:, kc * NP + j * 128:kc * NP + (j + 1) * 128], rhs=HT[:, kc, :], start=(kc == 0), stop=(kc == KC - 1))),
                                reads=[("wp", slot), "HT"], writes=[("ps", b)])
                    i = rot("stf", 5)
                    P.add("scalar", (lambda e, i=i, bg=bg: e.activation(out=STF[i][:], in_=PSB[bg][:], func=AF.Silu)), reads=[("ps", bg)], writes=[("stf", i)])
                    P.add("vector", (lambda e, i=i, bu=bu, hb=hb: e.tensor_tensor(out=BIGb[:, hb * TB:(hb + 1) * TB], in0=PSB[bu][:], in1=STF[i][:], op=ALU.mult)),
                          reads=[("ps", bu), ("stf", i)], writes=[("BIG", hb)])

            def epi_d(nb, pt, pkey, t0=t0):
                i = rot("stf", 5)
                P.add("sync", (lambda e: e.dma_start(out=STF[i][0:64, :], in_=x1[nb * 64:(nb + 1) * 64, t0:t0 + TB])),
                      reads=[("x1", nb // 2, t0)], writes=[("stf", i)], dma_key=("stfl", i))
                P.add("vector", (lambda e: e.tensor_tensor(out=STF[i][0:64, :], in0=pt[0:64, :], in1=STF[i][0:64, :], op=ALU.add)), reads=[pkey, ("stf", i)], writes=[("stf", i)])
                store("sync", dst[nb * 64:(nb + 1) * 64, t0:t0 + TB], STF[i][0:64, :], ("stf", i), ("dst", nb, t0))
            gemm("w_d", l, (lambda kc: BIGb[:, kc * TB:(kc + 1) * TB]), (lambda kc: ("BIG", kc)), KF, 64, D // 64, epi_d)
            P.barrier()

    final = []
    for l in range(depth):
        src = xin if l == 0 else x2
        dst = yout if l == depth - 1 else x2
        Wl = W[l]
        P.add("sync", lambda e, Wl=Wl: e.dma_start(out=par[:], in_=Wl["par"][:, :]), writes=["par"], dma_key="par")
        cast_weights(l, ["w_in"])

        for tb in range(NTB):
            t0 = tb * TB
            rmsnorm_block(src, tb, PO["norm_mix"])

            def epi(nb, pt, pkey, t0=t0):
                i = rot("stg", 4)
                if nb < NB_MIX:
                    P.add("vector", (lambda e: e.tensor_copy(out=STG[i][:], in_=pt[:])), reads=[pkey], writes=[("stg", i)])
                    store("sync", proj[nb * 128:(nb + 1) * 128, t0:t0 + TB], STG[i][:], ("stg", i), ("proj", nb, t0))
                else:
                    g = nb - NB_MIX
                    P.add("scalar", (lambda e: e.activation(out=STG[i][:], in_=pt[:], func=AF.Sigmoid)), reads=[pkey], writes=[("stg", i)])
                    store("sync", gates[g * 128:(g + 1) * 128, t0:t0 + TB], STG[i][:], ("stg", i), ("gates", g, t0))
            gemm("w_in", l, (lambda kc: HT[:, kc, :]), "HT", KC, NP, NPI, epi)
        if stop_after == "p1":
            break
        P.barrier()
        SP = TB
        NSP = S // SP
        V = mybir

        def gelu(X, G, T):
            return [("vector", lambda e: e.tensor_tensor(out=T, in0=X, in1=X, op=ALU.mult)),
                    ("vector", lambda e: e.tensor_scalar(out=T, in0=T, scalar1=0.044715, scalar2=1.0, op0=ALU.mult, op1=ALU.add)),
                    ("vector", lambda e: e.tensor_tensor(out=T, in0=T, in1=X, op=ALU.mult)),
                    ("scalar", lambda e: e.activation(out=T, in_=T, func=AF.Sigmoid, scale=1.5957691216057308)),
                    ("vector", lambda e: e.tensor_tensor(out=G, in0=X, in1=T, op=ALU.mult))]

        def ld(dst, dkey, src_ap, skey, q="sync"):
            return P.add(q, (lambda e: e.dma_start(out=dst, in_=src_ap)), reads=[skey], writes=[dkey], dma_key=("ld", dkey))

        soff[0] = 0
        GV = [sf(TB) for _ in range(8)]
        VB = [sh(TB) for _ in range(8)]
        TMP = sf(TB); MEAN = sf(TB); RST = sf(TB)
        for tb in range(NTB):
            t0 = tb * TB
            for c in range(8):
                nb = c * 10 + 4
                ld(VB[c], ("VB", c), proj[nb * 128:(nb + 1) * 128, t0:t0 + TB], ("proj", nb, t0))
                for (en, fn) in gelu(VB[c], GV[c], TMP):
                    P.add(en, fn, reads=[("VB", c), ("GV", c), "TMP"], writes=[("GV", c), "TMP"])
                P.add("tensor", (lambda e, c=c: e.matmul(PSB[4][:], lhsT=ones_f[:], rhs=GV[c], start=(c == 0), stop=(c == 7))),
                      reads=[("GV", c), "ones_f"], writes=[("ps", 4)])
                P.add("scalar", (lambda e, c=c: e.activation(out=TMP, in_=GV[c], func=AF.Square)), reads=[("GV", c)], writes=["TMP"])
                P.add("tensor", (lambda e, c=c: e.matmul(PSB[5][:], lhsT=ones_f[:], rhs=TMP, start=(c == 0), stop=(c == 7))),
                      reads=["TMP", "ones_f"], writes=[("ps", 5)])
            P.add("vector", lambda e: e.tensor_scalar(out=MEAN, in0=PSB[4][:], scalar1=1.0 / WM, scalar2=None, op0=ALU.mult), reads=[("ps", 4)], writes=["MEAN"])
            P.add("vector", lambda e: e.tensor_tensor(out=TMP, in0=MEAN, in1=MEAN, op=ALU.mult), reads=["MEAN"], writes=["TMP"])
            P.add("vector", lambda e: e.scalar_tensor_tensor(out=RST, in0=PSB[5][:], scalar=1.0 / WM, in1=TMP, op0=ALU.mult, op1=ALU.subtract),
                  reads=[("ps", 5), "TMP"], writes=["RST"])
            P.add("vector", lambda e: e.tensor_scalar(out=RST, in0=RST, scalar1=EPS, scalar2=None, op0=ALU.add), reads=["RST"], writes=["RST"])
            P.add("scalar", lambda e: e.activation(out=RST, in_=RST, func=AF.Sqrt), reads=["RST"], writes=["RST"])
            P.add("vector", lambda e: e.reciprocal(out=RST, in_=RST), reads=["RST"], writes=["RST"])
            for c in range(8):
                nb = c * 10 + 4
                P.add("vector", (lambda e, c=c: e.tensor_tensor(out=GV[c], in0=GV[c], in1=MEAN, op=ALU.subtract)), reads=[("GV", c), "MEAN"], writes=[("GV", c)])
                P.add("vector", (lambda e, c=c: e.scalar_tensor_tensor(out=VB[c], in0=GV[c], scalar=par[:, PO["sgu_norm"] + c:PO["sgu_norm"] + c + 1],
                                                                        in1=RST, op0=ALU.mult, op1=ALU.mult)),
                      reads=[("GV", c), "RST", "par"], writes=[("VB", c)])
                P.add("sync", (lambda e, c=c, nb=nb, t0=t0: e.dma_start(out=proj[nb * 128:(nb + 1) * 128, t0:t0 + TB], in_=VB[c])),
                      reads=[("VB", c)], writes=[("proj", nb, t0)], dma_key=("st", ("VB", c)))
        P.barrier()
        if stop_after == "p1b":
            break

        soff[0] = 0
        WBT = sh(8 * 128)
        BHI = sh(1024); BLO = sh(1024)
        BF = sf(1024); BF2 = sf(1024)
        LB = sh(SP); LC = sh(SP); LX = sh(SP); LU = sh(SP); LV = sh(SP)
        CXE = sf(SP + 2); HAL = sf(2); ACC = sf(SP); YA = sh(SP)
        GU = sf(SP); TG = sf(SP); VT_ = sh(128); YB = sh(SP)
        for c in range(8):
            P.add("sync", (lambda e, c=c, Wl=Wl: e.dma_start(out=TG[:, 0:128], in_=Wl["sguw"][c])), writes=["TG"], dma_key="sguw")
            P.add("vector", (lambda e, c=c: e.tensor_tensor(out=WBT[:, c * 128:(c + 1) * 128], in0=TG[:, 0:128], in1=LE_f, op=ALU.mult)),
                  reads=["TG", "cf32"], writes=["WBT"])
        P.add("sync", lambda e, Wl=Wl: e.dma_start(out=BF[0:1, :], in_=Wl["sgub"][:, :]), writes=["BF"], dma_key="sgub")
        P.add("vector", lambda e: e.tensor_copy(out=BHI[0:1, :], in_=BF[0:1, :]), reads=["BF"], writes=["BHI"])
        P.add("vector", lambda e: e.tensor_copy(out=BF2[0:1, :], in_=BHI[0:1, :]), reads=["BHI"], writes=["BF2"])
        P.add("vector", lambda e: e.tensor_tensor(out=BF2[0:1, :], in0=BF[0:1, :], in1=BF2[0:1, :], op=ALU.subtract), reads=["BF", "BF2"], writes=["BF2"])
        P.add("vector", lambda e: e.tensor_copy(out=BLO[0:1, :], in_=BF2[0:1, :]), reads=["BF2"], writes=["BLO"])
        for c in range(8):
            ca = PO["conv_a"] + 3 * c
            for i in range(NSP):
                t0 = i * SP
                for (tl, k, j) in ((LB, "LB", 0), (LC, "LC", 1), (LX, "LX", 2)):
                    nb = c * 10 + j
                    ld(tl, k, proj[nb * 128:(nb + 1) * 128, t0:t0 + SP], ("proj", nb, t0))
                if i == 0:
                    P.add("vector", lambda e: e.memset(CXE[:, 0:2], 0.0), writes=["CXE"])
                else:
                    P.add("vector", lambda e: e.tensor_copy(out=CXE[:, 0:2], in_=HAL), reads=["HAL"], writes=["CXE"])
                P.add("vector", lambda e: e.tensor_tensor(out=CXE[:, 2:2 + SP], in0=LC, in1=LX, op=ALU.mult), reads=["LC", "LX"], writes=["CXE"])
                P.add("vector", (lambda e, ca=ca: e.tensor_scalar(out=ACC, in0=CXE[:, 0:SP], scalar1=par[:, ca:ca + 1], scalar2=None, op0=ALU.mult)),
                      reads=["CXE", "par"], writes=["ACC"])
                for k in (1, 2):
                    P.add("vector", (lambda e, ca=ca, k=k: e.scalar_tensor_tensor(out=ACC, in0=CXE[:, k:k + SP], scalar=par[:, ca + k:ca + k + 1],
                                                                                   in1=ACC, op0=ALU.mult, op1=ALU.add)),
                          reads=["CXE", "par", "ACC"], writes=["ACC"])
                P.add("vector", lambda e: e.tensor_copy(out=HAL, in_=CXE[:, SP:SP + 2]), reads=["CXE"], writes=["HAL"])
                P.add("vector", lambda e: e.tensor_tensor(out=YA, in0=ACC, in1=LB, op=ALU.mult), reads=["ACC", "LB"], writes=["YA"])
                store("sync", ymix[(0 * 8 + c) * 128:(0 * 8 + c + 1) * 128, t0:t0 + SP], YA, "YA", ("ymix", c, t0))
                ld(LU, "LU", proj[(c * 10 + 3) * 128:(c * 10 + 4) * 128, t0:t0 + SP], ("proj", c * 10 + 3, t0))
                ld(LV, "LV", proj[(c * 10 + 4) * 128:(c * 10 + 5) * 128, t0:t0 + SP], ("proj", c * 10 + 4, t0))
                for (en, fn) in gelu(LU, GU, TG):
                    P.add(en, fn, reads=["LU", "GU", "TG"], writes=["GU", "TG"])
                for j in range(SP // 128):
                    P.add("tensor", (lambda e, j=j: e.transpose(PT[:, 0:128], LV[:, j * 128:(j + 1) * 128], ident_b)),
                          reads=["LV", "cb16"], writes=["PT"])
                    P.add("vector", lambda e: e.tensor_copy(out=VT_, in_=PT[:, 0:128]), reads=["PT"], writes=["VT_"])
                    P.add("tensor", (lambda e, c=c: e.matmul(PSB[4][:, 0:128], lhsT=VT_, rhs=WBT[:, c * 128:(c + 1) * 128], start=True, stop=False)),
                          reads=["VT_", "WBT"], writes=[("ps", 4)])
                    P.add("tensor", (lambda e, c=c: e.matmul(PSB[4][:, 0:128], lhsT=ones_b[0:1, :], rhs=BHI[0:1, c * 128:(c + 1) * 128], start=False, stop=False)),
                          reads=["BHI", "ones_b"], writes=[("ps", 4)])
                    P.add("tensor", (lambda e, c=c: e.matmul(PSB[4][:, 0:128], lhsT=ones_b[0:1, :], rhs=BLO[0:1, c * 128:(c + 1) * 128], start=False, stop=True)),
                          reads=["BLO", "ones_b"], writes=[("ps", 4)])
                    P.add("vector", (lambda e, j=j: e.tensor_tensor(out=YB[:, j * 128:(j + 1) * 128], in0=PSB[4][:, 0:128], in1=GU[:, j * 128:(j + 1) * 128], op=ALU.mult)),
                          reads=[("ps", 4), "GU"], writes=["YB"])
                store("sync", ymix[(1 * 8 + c) * 128:(1 * 8 + c + 1) * 128, t0:t0 + SP], YB, "YB", ("ymix", 8 + c, t0))
        P.barrier()
        mix_c(l, Wl)
        P.barrier()
        mix_d(l, Wl)
        P.barrier()
        if stop_after == "p2":
            break
        phase3(l, Wl, src, dst)
        P.barrier()
    fw = [op for k, op in P.last_w.items() if op.is_dma]
    with nc.Block() as block:
        P.emit(block, fw)
    return nc, es


def prep_weights(l, inp):
    perm = in_col_perm()
    w_in = inp["w_in"][l]
    wp = np.zeros((D, len(perm)), np.float32)
    m = perm >= 0
    wp[:, m] = w_in[:, perm[m]]
    d = {"w_in%d" % l: panelize(wp, NP)}
    del wp
    if "w_branch" in inp:
        d["w_br%d" % l] = panelize(inp["w_branch"][l].reshape(D, D), NP)
        d["w_out%d" % l] = panelize(inp["w_out"][l], NP)
    if "w_ffn_gate" in inp:
        d["w_g%d" % l] = panelize(inp["w_ffn_gate"][l], NP)
        d["w_u%d" % l] = panelize(inp["w_ffn_up"][l], NP)
        d["w_d%d" % l] = panelize(inp["w_ffn_down"][l], 64)
    d.update(small_inputs(l, inp))
    return d


def kernel(**inputs):
    inp = {k: np.asarray(v) for k, v in inputs.items()}
    x = inp["x"]
    Bsz, S, _ = x.shape
    depth = inp["w_in"].shape[0]
    DFF = inp["w_ffn_gate"].shape[2]
    nc, es = build_nc(S, DFF, depth)
    shared = {"consts": build_consts()}
    for l in range(depth):
        shared.update(prep_weights(l, inp))
    in_maps = []
    for b in range(Bsz):
        d = dict(shared)
        d["xT"] = np.ascontiguousarray(x[b].T)
        in_maps.append(d)
    res = run_bass_kernel_spmd(nc, in_maps, core_ids=list(range(Bsz)))
    out = np.stack([np.ascontiguousarray(res.results[b]["yT"].T) for b in range(Bsz)], axis=0)
    return out.astype(np.float32)
```

```python
import math
from contextlib import ExitStack
import numpy as np
import concourse.bass as bass
import concourse.mybir as mybir
from concourse.bass_utils import run_bass_kernel_spmd

F32 = mybir.dt.float32
BF16 = mybir.dt.bfloat16
ALU = mybir.AluOpType
AF = mybir.ActivationFunctionType

D = 4096
KC = D // 128
WM = 1024
EPS = 1e-6
NB_MIX = 86
NB_GATE = 128
NP = 256
SAME_ENG_SYNC = True


class Op:
    __slots__ = ("eng", "fn", "deps", "needed", "mark", "sem", "is_dma", "idx")

    def __init__(self, eng, fn, is_dma):
        self.eng = eng; self.fn = fn; self.deps = []; self.needed = False
        self.mark = None; self.sem = None; self.is_dma = is_dma


class Plan:
    ENGS = ("sync", "tensor", "vector", "scalar", "gpsimd")

    def __init__(self, nc, es):
        self.nc = nc; self.es = es
        self.ops = {e: [] for e in self.ENGS}
        self.last_w = {}; self.readers = {}
        self.esem = {e: es.enter_context(nc.semaphore("es_" + e)) for e in self.ENGS}
        self.dsem = {}; self.dcount = {}
        self.pend = {e: [] for e in self.ENGS}; self.dmas = []

    def barrier(self):
        deps = [o for e in self.ENGS for o in reversed(self.ops[e]) if not o.is_dma][:0]
        for e in self.ENGS:
            for o in reversed(self.ops[e]):
                if not o.is_dma:
                    deps.append(o); break
        deps = deps + self.dmas
        self.dmas = []
        for e in self.ENGS:
            self.pend[e] = list(deps)

    def add(self, eng, fn, reads=(), writes=(), dma_key=None):
        op = Op(eng, fn, dma_key is not None)
        deps = list(self.pend[eng]); self.pend[eng] = []
        for k in reads:
            w = self.last_w.get(k)
            if w is not None: deps.append(w)
        for k in writes:
            w = self.last_w.get(k)
            if w is not None: deps.append(w)
            deps.extend(self.readers.get(k, ()))
        best = {}
        for d in deps:
            if d is op: continue
            if (not d.is_dma) and d.eng == eng and not op.is_dma and (eng == "tensor" or not SAME_ENG_SYNC):
                continue
            key = ("d", id(d.sem)) if d.is_dma else ("e", d.eng)
            rank = d.mark if d.is_dma else d.idx
            if key not in best or rank > best[key][0]:
                best[key] = (rank, d)
        for (_, d) in best.values():
            d.needed = True
            op.deps.append(d)
        for k in reads:
            self.readers.setdefault(k, []).append(op)
        for k in writes:
            self.last_w[k] = op; self.readers[k] = []
        if op.is_dma:
            if dma_key not in self.dsem:
                self.dsem[dma_key] = self.es.enter_context(self.nc.semaphore("ds%d" % len(self.dsem)))
                self.dcount[dma_key] = 0
            self.dcount[dma_key] += 16
            op.sem = self.dsem[dma_key]; op.mark = self.dcount[dma_key]; op.needed = True
            self.dmas.append(op)
        op.idx = len(self.ops[eng])
        self.ops[eng].append(op)
        return op

    def emit(self, block, final_waits):
        for e in self.ENGS:
            c = 0
            for op in self.ops[e]:
                if not op.is_dma and op.needed:
                    c += 1; op.mark = c; op.sem = self.esem[e]
        plan = self

        def run(engname):
            def body(eng):
                waited = {}
                for op in plan.ops[engname]:
                    for d in op.deps:
                        key = id(d.sem)
                        if waited.get(key, 0) >= d.mark: continue
                        waited[key] = d.mark
                        eng.wait_ge(d.sem, d.mark)
                    ins = op.fn(eng)
                    if op.is_dma:
                        ins.then_inc(op.sem, 16)
                    elif op.needed:
                        ins.then_inc(op.sem, 1)
                if engname == "sync":
                    for d in final_waits:
                        eng.wait_ge(d.sem, d.mark)
            return body
        block.sync(run("sync")); block.tensor(run("tensor")); block.vector(run("vector"))
        block.scalar(run("scalar")); block.gpsimd(run("gpsimd"))


def in_col_perm():
    o = {}
    names = ["a_b", "a_c", "a_x", "b_u", "b_v", "c_q", "c_k", "c_v", "d_z"]
    off = 0
    for n in names:
        o[n] = off; off += WM
    o["d_xs"] = off; off += WM
    o["d_B"] = off; off += 256
    o["d_C"] = off; off += 256
    o["d_dt"] = off; off += 16
    o["gates"] = off
    cols = []
    for c in range(8):
        for n in ["a_b", "a_c", "a_x", "b_u", "b_v", "c_q", "c_k", "c_v", "d_z", "d_xs"]:
            cols.extend(range(o[n] + c * 128, o[n] + (c + 1) * 128))
    for g in range(2):
        cols.extend(range(o["d_B"] + g * 128, o["d_B"] + (g + 1) * 128))
        cols.extend(range(o["d_C"] + g * 128, o["d_C"] + (g + 1) * 128))
    cols.extend(range(o["d_dt"], o["d_dt"] + 16)); cols.extend([-1] * 112)
    cols.extend([-1] * 128)
    assert len(cols) == NB_MIX * 128
    cols.extend(range(o["gates"], o["gates"] + 4 * D))
    return np.array(cols, dtype=np.int64)


def panelize(w, npw):
    K, N = w.shape
    assert K % 128 == 0 and N % npw == 0
    a = w.reshape(K // 128, 128, N // npw, npw)
    return np.ascontiguousarray(a.transpose(2, 1, 0, 3)).reshape(N // npw, 128, (K // 128) * npw)


def vec_cols(v):
    return np.ascontiguousarray(v.reshape(-1, 128).T)


PO = {}
def _po():
    off = 0
    for name, n in [("norm_mix", 32), ("norm_ffn", 32), ("conv_a", 24), ("sgu_norm", 8), ("qn", 1), ("kn", 1),
                    ("cw_x", 32), ("cb_x", 8), ("cw_bc", 16), ("cb_bc", 4), ("ssm_norm", 8), ("dskip", 8)]:
        PO[name] = off; off += n
    PO["_n"] = off
_po()


def build_params(l, inp):
    p = np.zeros((128, PO["_n"]), np.float32)
    p[:, PO["norm_mix"]:PO["norm_mix"] + 32] = vec_cols(inp["norm_mix"][l])
    p[:, PO["norm_ffn"]:PO["norm_ffn"] + 32] = vec_cols(inp["norm_ffn"][l])
    ca = inp["conv_a"][l]
    for c in range(8):
        p[:, PO["conv_a"] + 3 * c:PO["conv_a"] + 3 * c + 3] = ca[:, c * 128:(c + 1) * 128].T
    p[:, PO["sgu_norm"]:PO["sgu_norm"] + 8] = vec_cols(inp["sgu_norm"][l])
    p[:, PO["qn"]] = inp["q_norm"][l]; p[:, PO["kn"]] = inp["k_norm"][l]
    cw = inp["ssm_conv_w"][l]; cb = inp["ssm_conv_b"][l]
    for c in range(8):
        p[:, PO["cw_x"] + 4 * c:PO["cw_x"] + 4 * c + 4] = cw[:, c * 128:(c + 1) * 128].T
        p[:, PO["cb_x"] + c] = cb[c * 128:(c + 1) * 128]
    for g in range(2):
        for j in range(2):
            o = 1024 + j * 256 + g * 128
            i = g * 2 + j
            p[:, PO["cw_bc"] + 4 * i:PO["cw_bc"] + 4 * i + 4] = cw[:, o:o + 128].T
            p[:, PO["cb_bc"] + i] = cb[o:o + 128]
    p[:, PO["ssm_norm"]:PO["ssm_norm"] + 8] = vec_cols(inp["ssm_norm"][l])
    p[:, PO["dskip"]:PO["dskip"] + 8] = np.repeat(inp["ssm_d"][l].reshape(8, 2), 64, axis=1).T
    return p


def small_inputs(l, inp):
    return {"par%d" % l: build_params(l, inp),
            "dtb%d" % l: np.concatenate([inp["ssm_dt_bias"][l], inp["ssm_a_log"][l], inp["ssm_d"][l]])[None, :].astype(np.float32),
            "sguw%d" % l: np.ascontiguousarray(inp["sgu_w"][l].transpose(0, 2, 1)),
            "sgub%d" % l: np.ascontiguousarray(inp["sgu_b"][l].reshape(1, 1024))}


def build_consts():
    s = np.arange(128)[:, None]
    t = np.arange(512)[None, :]
    c = np.zeros((128, 4 * 512 + 4 * 128), np.float32)
    for r in range(4):
        c[:, r * 512:(r + 1) * 512] = ((r * 128 + s) < t)
    t1 = np.arange(128)[None, :]
    o = 4 * 512
    c[:, o:o + 128] = (s <= t1)
    c[:, o + 128:o + 256] = -1.0 * (s >= t1)
    c[:, o + 256:o + 384] = -1.0
    c[:, o + 384:o + 512] = np.eye(128)
    return c


def build_nc(S, DFF, depth, stop_after=None, dbg=False):
    TB = 512
    NTB = S // TB
    KF = DFF // 128
    nc = bass.Bass("TRN2", target_bir_lowering=False)
    es = ExitStack()
    dr = lambda name, shape, dt, kind="Internal": nc.dram_tensor(name, shape, dt, kind=kind).ap()
    xin = dr("xT", [D, S], F32, "ExternalInput")
    yout = dr("yT", [D, S], F32, "ExternalOutput")
    consts_d = dr("consts", [128, 4 * 512 + 512], F32, "ExternalInput")
    NPI = (NB_MIX + NB_GATE) // 2
    W = []
    need = {"p1": ("w_in",), "p2": ("w_in",), "p3": ("w_in", "w_br", "w_out")}.get(stop_after, ("w_in", "w_br", "w_out", "w_g", "w_u", "w_d"))
    ikind = "ExternalOutput" if dbg else "Internal"
    for l in range(depth):
        W.append(dict(
            par=dr("par%d" % l, [128, PO["_n"]], F32, "ExternalInput"),
            dtb=dr("dtb%d" % l, [1, 48], F32, "ExternalInput"),
            sguw=dr("sguw%d" % l, [8, 128, 128], F32, "ExternalInput"),
            sgub=dr("sgub%d" % l, [1, 1024], F32, "ExternalInput"),
        ))
        wshapes = dict(w_in=[NPI, 128, KC * NP], w_br=[D // NP, 128, KC * NP], w_out=[D // NP, 128, KC * NP],
                       w_g=[DFF // NP, 128, KC * NP], w_u=[DFF // NP, 128, KC * NP], w_d=[D // 64, 128, KF * 64])
        for k in need:
            W[l][k] = dr("%s%d" % (k, l), wshapes[k], F32, "ExternalInput")
            W[l][k + "_b"] = dr("%s%d_b" % (k, l), wshapes[k], BF16)
    x1 = dr("x1", [D, S], F32)
    x2 = dr("x2", [D, S], F32)
    proj = dr("proj", [NB_MIX * 128, S], BF16, ikind)
    gates = dr("gates", [NB_GATE * 128, S], BF16, ikind)
    ymix = dr("ymix", [32 * 128, S], BF16, ikind)
    dbg_d = dr("dbg", [128, 2048], F32, "ExternalOutput") if dbg else None

    sb = lambda name, shape, dt: es.enter_context(nc.sbuf_tensor(name, shape, dt))
    ps = lambda name, shape, dt=F32: es.enter_context(nc.psum_tensor(name, shape, dt))
    P = Plan(nc, es)

    cb16 = sb("consts_b", [128, 4 * 512 + 512], BF16)
    cf32 = sb("consts_f", [128, 512], F32)
    ones_f = sb("ones_f", [128, 128], F32)
    ones_b = sb("ones_b", [128, 128], BF16)
    par = sb("par_s", [128, PO["_n"]], F32)
    CO = 4 * 512
    maskLT = lambda r: cb16[:, r * 512:(r + 1) * 512]
    LE_f = cf32[:, 0:128]
    LE_b = cb16[:, CO:CO + 128]
    NU_b = cb16[:, CO + 128:CO + 256]
    NEG1_b = cb16[:, CO + 256:CO + 384]
    ident_b = cb16[:, CO + 384:CO + 512]
    ident_f = cf32[:, 384:512]

    HT = sb("HT", [128, KC, TB], BF16)
    WPN = KC * NP
    WP = [sb("WP%d" % i, [128, WPN], BF16) for i in range(3)]
    BIGF = max(KF, 56) * TB // 2
    BIG = sb("BIG", [128, BIGF], F32)
    BIGb = BIG[:, :].bitcast(BF16)
    STG = [sb("STG%d" % i, [128, TB], BF16) for i in range(4)]
    GT = [sb("GT%d" % i, [128, TB], BF16) for i in range(6)]
    STF = [sb("STF%d" % i, [128, TB], F32) for i in range(5)]
    RS = sb("RS", [128, TB], F32)
    PSB = [ps("PS%d" % i, [128, TB]) for i in range(7)]
    PT = ps("PT", [128, 1024], BF16)
    soff = [0]

    def sf(n):
        a = BIG[:, soff[0]:soff[0] + n]; soff[0] += n; assert soff[0] <= BIGF; return a

    def sh(n):
        m = (n + 1) // 2
        a = BIG[:, soff[0]:soff[0] + m].bitcast(BF16); soff[0] += m; assert soff[0] <= BIGF; return a

    cnt = {"wp": 0, "ps": 0, "stg": 0, "stf": 0, "gt": 0}

    def rot(name, n):
        i = cnt[name] % n; cnt[name] += 1; return i

    for q in range(5):
        P.add("sync", (lambda e, q=q: e.dma_start(out=STF[q][:], in_=consts_d[:, q * 512:(q + 1) * 512])), writes=[("stf", q)], dma_key=("stfl", q))
        P.add("vector", (lambda e, q=q: e.tensor_copy(out=cb16[:, q * 512:(q + 1) * 512], in_=STF[q][:])), reads=[("stf", q)], writes=["cb16"])
    P.add("vector", lambda e: e.tensor_copy(out=cf32[:], in_=STF[4][:]), reads=[("stf", 4)], writes=["cf32"])
    P.add("vector", lambda e: e.memset(ones_f[:], 1.0), writes=["ones_f"])
    P.add("vector", lambda e: e.memset(ones_b[:], 1.0), writes=["ones_b"])

    def cast_weights(l, names):
        for k in names:
            src = W[l][k]; dst = W[l][k + "_b"]
            for pi in range(src.shape[0]):
                P.add("gpsimd", (lambda e, s=src, d=dst, pi=pi: e.dma_start(out=d[pi], in_=s[pi])),
                      writes=[(k + "_b", l, pi)], dma_key=("cast", pi % 8))

    def gemm(wname, l, act, act_key, kc_n, npw, n_panels, epilogue, panel_ok=None):
        wb = W[l][wname + "_b"]
        per = max(npw // 128, 1)
        for pi in range(n_panels):
            if panel_ok is not None and not panel_ok(pi): continue
            slot = rot("wp", 3)
            wp = WP[slot]
            P.add("sync", (lambda e, wp=wp, pi=pi: e.dma_start(out=wp[:, 0:kc_n * npw], in_=wb[pi])),
                  reads=[(wname + "_b", l, pi)], writes=[("wp", slot)], dma_key=("wp", slot))
            for j in range(per):
                b = rot("ps", 4)
                pt = PSB[b]
                for kc in range(kc_n):
                    P.add("tensor", (lambda e, pt=pt, wp=wp, kc=kc, j=j: e.matmul(
                        pt[0:min(npw, 128), :], lhsT=wp[:, kc * npw + j * 128: kc * npw + j * 128 + min(npw, 128)], rhs=act(kc),
                        start=(kc == 0), stop=(kc == kc_n - 1))),
                        reads=[("wp", slot), act_key(kc) if callable(act_key) else act_key], writes=[("ps", b)])
                epilogue(pi * per + j, pt, ("ps", b))

    def store(eng_q, dst_ap, src_tile, src_key, dst_key):
        return P.add(eng_q, (lambda e: e.dma_start(out=dst_ap, in_=src_tile)), reads=[src_key], writes=[dst_key],
                     dma_key=("st", src_key))

    def rmsnorm_block(src, tb, gcol, xkey=None):
        t0 = tb * TB
        pt = PSB[4]
        for kc in range(KC):
            i = rot("stf", 5)
            P.add("sync", (lambda e, i=i, kc=kc: e.dma_start(out=STF[i][:], in_=src[kc * 128:(kc + 1) * 128, t0:t0 + TB])),
                  reads=([xkey(kc)] if xkey else []), writes=[("stf", i)], dma_key=("stfl", i))
            P.add("scalar", (lambda e, i=i: e.activation(out=STF[i][:], in_=STF[i][:], func=AF.Square)),
                  reads=[("stf", i)], writes=[("stf", i)])
            P.add("tensor", (lambda e, i=i, kc=kc: e.matmul(pt[:], lhsT=ones_f[:], rhs=STF[i][:], start=(kc == 0), stop=(kc == KC - 1))),
                  reads=[("stf", i), "ones_f"], writes=[("ps", 4)])
        P.add("vector", lambda e: e.tensor_scalar(out=RS[:], in0=pt[:], scalar1=1.0 / D, scalar2=EPS, op0=ALU.mult, op1=ALU.add),
              reads=[("ps", 4)], writes=["RS"])
        P.add("scalar", lambda e: e.activation(out=RS[:], in_=RS[:], func=AF.Sqrt), reads=["RS"], writes=["RS"])
        P.add("vector", lambda e: e.reciprocal(out=RS[:], in_=RS[:]), reads=["RS"], writes=["RS"])
        for kc in range(KC):
            i = rot("stf", 5)
            P.add("sync", (lambda e, i=i, kc=kc: e.dma_start(out=STF[i][:], in_=src[kc * 128:(kc + 1) * 128, t0:t0 + TB])),
                  reads=([xkey(kc)] if xkey else []), writes=[("stf", i)], dma_key=("stfl", i))
            P.add("vector", (lambda e, kc=kc, i=i: e.scalar_tensor_tensor(
                out=HT[:, kc, :], in0=STF[i][:], scalar=par[:, gcol + kc:gcol + kc + 1], in1=RS[:], op0=ALU.mult, op1=ALU.mult)),
                reads=[("stf", i), "RS", "par"], writes=["HT"])

    SP = TB
    NSP = S // SP

    def ld(dst, dkey, src_ap, skey, q="sync"):
        return P.add(q, (lambda e: e.dma_start(out=dst, in_=src_ap)), reads=[skey], writes=[dkey], dma_key=("ld", dkey))

    def mix_c(l, Wl):
        soff[0] = 0
        QN = sh(S); KN = sh(S); VT = sh(S)
        LQ = sh(SP); SQ = sf(SP); RQ = sf(SP)
        E = [sf(SP) for _ in range(2)]
        SPT = [sh(SP) for _ in range(2)]
        SPS = sh(SP)
        ATT = [sh(SP) for _ in range(2)]
        YC = sh(SP)
        GQ = sf(2)
        scale = 1.0 / math.sqrt(128.0)
        P.add("vector", lambda e: e.tensor_scalar(out=GQ[:, 0:1], in0=par[:, PO["qn"]:PO["qn"] + 1], scalar1=scale, scalar2=None, op0=ALU.mult),
              reads=["par"], writes=["GQ"])
        P.add("vector", lambda e: e.tensor_copy(out=GQ[:, 1:2], in_=par[:, PO["kn"]:PO["kn"] + 1]), reads=["par"], writes=["GQ"])
        for c in range(8):
            for i in range(NSP):
                t0 = i * SP
                for (j, dstt, dk, gi) in ((5, QN, "QN", 0), (6, KN, "KN", 1)):
                    nb = c * 10 + j
                    ld(LQ, "LQ", proj[nb * 128:(nb + 1) * 128, t0:t0 + SP], ("proj", nb, t0))
                    P.add("vector", lambda e: e.tensor_tensor(out=SQ, in0=LQ, in1=LQ, op=ALU.mult), reads=["LQ"], writes=["SQ"])
                    P.add("tensor", lambda e: e.matmul(PSB[4][:], lhsT=ones_f[:], rhs=SQ, start=True, stop=True), reads=["SQ", "ones_f"], writes=[("ps", 4)])
                    P.add("vector", lambda e: e.tensor_scalar(out=RQ, in0=PSB[4][:], scalar1=1.0 / 128, scalar2=EPS, op0=ALU.mult, op1=ALU.add),
                          reads=[("ps", 4)], writes=["RQ"])
                    P.add("scalar", lambda e: e.activation(out=RQ, in_=RQ, func=AF.Sqrt), reads=["RQ"], writes=["RQ"])
                    P.add("vector", lambda e: e.reciprocal(out=RQ, in_=RQ), reads=["RQ"], writes=["RQ"])
                    P.add("vector", (lambda e, dstt=dstt, gi=gi, t0=t0: e.scalar_tensor_tensor(out=dstt[:, t0:t0 + SP], in0=LQ, scalar=GQ[:, gi:gi + 1], in1=RQ,
                                                                                              op0=ALU.mult, op1=ALU.mult)),
                          reads=["LQ", "RQ", "GQ"], writes=[dk])
                nb = c * 10 + 7
                ld(LQ, "LQ", proj[nb * 128:(nb + 1) * 128, t0:t0 + SP], ("proj", nb, t0))
                for j in range(SP // 128):
                    P.add("tensor", (lambda e, j=j: e.transpose(PT[:, j * 128:(j + 1) * 128], LQ[:, j * 128:(j + 1) * 128], ident_b)),
                          reads=["LQ", "cb16"], writes=["PT"])
                P.add("vector", (lambda e, t0=t0: e.tensor_copy(out=VT[:, t0:t0 + SP], in_=PT[:, 0:SP])), reads=["PT"], writes=["VT"])
            for i in range(NSP):
                t0 = i * SP
                nq = SP // 128
                kbs = list(range(i * nq + nq - 1, -1, -1))
                po = PSB[6]
                for n, kb in enumerate(kbs):
                    r = kb - i * nq
                    a = n % 2
                    pa = PSB[a]; pb = PSB[2 + a]
                    P.add("tensor", (lambda e, pa=pa, kb=kb, t0=t0: e.matmul(pa[:], lhsT=KN[:, kb * 128:(kb + 1) * 128], rhs=QN[:, t0:t0 + SP], start=True, stop=True)),
                          reads=["KN", "QN"], writes=[("ps", a)])
                    P.add("scalar", (lambda e, pa=pa, a=a: e.activation(out=E[a], in_=pa[:], func=AF.Exp)), reads=[("ps", a)], writes=[("E", a)])
                    P.add("scalar", (lambda e, a=a: e.activation(out=SPT[a], in_=E[a], func=AF.Ln, bias=1.0)), reads=[("E", a)], writes=[("SPT", a)])
                    if r >= 0:
                        P.add("gpsimd", (lambda e, a=a, r=r: e.tensor_tensor(out=SPT[a], in0=SPT[a], in1=maskLT(r), op=ALU.mult)),
                              reads=[("SPT", a), "cb16"], writes=[("SPT", a)])
                    P.add("tensor", (lambda e, pb=pb, kb=kb, t0=t0: e.matmul(pb[:], lhsT=KN[:, kb * 128:(kb + 1) * 128], rhs=QN[:, t0:t0 + SP], start=True, stop=False)),
                          reads=["KN", "QN"], writes=[("ps", 2 + a)])
                    P.add("tensor", (lambda e, pb=pb, a=a, n=n: e.matmul(pb[:], lhsT=NU_b, rhs=SPT[a], start=False, stop=(n == 0))),
                          reads=[("SPT", a), "cb16"], writes=[("ps", 2 + a)])
                    if n > 0:
                        P.add("tensor", (lambda e, pb=pb: e.matmul(pb[:], lhsT=NEG1_b, rhs=SPS, start=False, stop=True)),
                              reads=["SPS", "cb16"], writes=[("ps", 2 + a)])
                    P.add("scalar", (lambda e, pb=pb, a=a: e.activation(out=ATT[a], in_=pb[:], func=AF.Exp)), reads=[("ps", 2 + a)], writes=[("ATT", a)])
                    if r >= 0:
                        P.add("gpsimd", (lambda e, a=a, r=r: e.tensor_tensor(out=ATT[a], in0=ATT[a], in1=maskLT(r), op=ALU.mult)),
                              reads=[("ATT", a), "cb16"], writes=[("ATT", a)])
                    P.add("tensor", (lambda e, kb=kb, a=a, n=n: e.matmul(po[:], lhsT=VT[:, kb * 128:(kb + 1) * 128], rhs=ATT[a], start=(n == 0), stop=(n == len(kbs) - 1))),
                          reads=["VT", ("ATT", a)], writes=[("ps", 6)])
                    if n == 0:
                        P.add("gpsimd", (lambda e, a=a: e.tensor_copy(out=SPS, in_=SPT[a])), reads=[("SPT", a)], writes=["SPS"])
                    elif n < len(kbs) - 1:
                        P.add("gpsimd", (lambda e, a=a: e.tensor_tensor(out=SPS, in0=SPS, in1=SPT[a], op=ALU.add)), reads=[("SPT", a), "SPS"], writes=["SPS"])
                P.add("vector", lambda e: e.tensor_copy(out=YC, in_=po[:]), reads=[("ps", 6)], writes=["YC"])
                store("sync", ymix[(16 + c) * 128:(16 + c + 1) * 128, t0:t0 + SP], YC, "YC", ("ymix", 16 + c, t0))

    def mix_d(l, Wl):
        soff[0] = 0
        LXS = sh(SP); LBm = sh(SP); LCm = sh(SP); LDT = sh(SP)
        XE = sf(SP + 3); BE = sf(SP + 3); CE = sf(SP + 3)
        HX = sf(3); HB = sf(3); HC = sf(3)
        XC = sh(SP); BC = sh(SP); CC = sh(SP); ACCD = sf(SP); YD = sh(SP)
        DTB = sf(48); DTR = sf(48); A2 = sf(16)
        DT2 = sf(2); DA = sf(2); CUMC = sf(2); CL = sf(2); DE = sf(2); CD = sf(2); TE = sf(2)
        DAB = [sf(128) for _ in range(2)]
        TD = [sf(128) for _ in range(2)]
        MH = [sh(128) for _ in range(2)]
        ECB = [sf(128) for _ in range(2)]
        CS = [sh(128) for _ in range(2)]
        XDTA = sh(128); XDTB = sh(128); XDE = sh(128); BT = sh(128)
        HF = sf(128); HTA = sh(128); HTB = sh(128)
        DBG = sf(2048) if dbg else None
        P.add("sync", lambda e: e.dma_start(out=DTR[0:1, :], in_=Wl["dtb"][:, :]), writes=["DTR"], dma_key="dtr")
        P.add("tensor", lambda e: e.matmul(PSB[0][:, 0:48], lhsT=ones_f[0:1, :], rhs=DTR[0:1, :], start=True, stop=True), reads=["DTR", "ones_f"], writes=[("ps", 0)])
        P.add("vector", lambda e: e.tensor_copy(out=DTB, in_=PSB[0][:, 0:48]), reads=[("ps", 0)], writes=["DTB"])
        P.add("scalar", lambda e: e.activation(out=A2, in_=DTB[:, 16:32], func=AF.Exp), reads=["DTB"], writes=["A2"])
        P.add("vector", lambda e: e.tensor_scalar(out=A2, in0=A2, scalar1=-1.0, scalar2=None, op0=ALU.mult), reads=["A2"], writes=["A2"])
        P.add("vector", lambda e: e.memset(XDTA, 0.0), writes=["XDTA"])
        P.add("vector", lambda e: e.memset(XDTB, 0.0), writes=["XDTB"])
        for c in range(8):
            g = c // 4
            h0 = 2 * c
            P.add("vector", lambda e: e.memset(HF, 0.0), writes=["HF"])
            P.add("vector", lambda e: e.memset(HTA, 0.0), writes=["HTA"])
            P.add("vector", lambda e: e.memset(HTB, 0.0), writes=["HTB"])
            for i in range(NSP):
                t0 = i * SP
                srcs = ((LXS, "LXS", c * 10 + 9, XE, "XE", HX, "HX", XC, "XC", PO["cw_x"] + 4 * c, PO["cb_x"] + c),
                        (LBm, "LBm", 80 + 2 * g, BE, "BE", HB, "HB", BC, "BC", PO["cw_bc"] + 4 * (2 * g), PO["cb_bc"] + 2 * g),
                        (LCm, "LCm", 81 + 2 * g, CE, "CE", HC, "HC", CC, "CC", PO["cw_bc"] + 4 * (2 * g + 1), PO["cb_bc"] + 2 * g + 1))
                for (LT, lk, nb, EXT, ek, HL, hk, OUT, ok, wc, bc) in srcs:
                    ld(LT, lk, proj[nb * 128:(nb + 1) * 128, t0:t0 + SP], ("proj", nb, t0))
                    if i == 0:
                        P.add("vector", (lambda e, EXT=EXT: e.memset(EXT[:, 0:3], 0.0)), writes=[ek])
                    else:
                        P.add("vector", (lambda e, EXT=EXT, HL=HL: e.tensor_copy(out=EXT[:, 0:3], in_=HL)), reads=[hk], writes=[ek])
                    P.add("vector", (lambda e, EXT=EXT, LT=LT: e.tensor_copy(out=EXT[:, 3:3 + SP], in_=LT)), reads=[lk], writes=[ek])
                    P.add("vector", (lambda e, EXT=EXT, wc=wc: e.tensor_scalar(out=ACCD, in0=EXT[:, 0:SP], scalar1=par[:, wc:wc + 1], scalar2=None, op0=ALU.mult)),
                          reads=[ek, "par"], writes=["ACCD"])
                    for k in (1, 2, 3):
                        P.add("vector", (lambda e, EXT=EXT, wc=wc, k=k: e.scalar_tensor_tensor(out=ACCD, in0=EXT[:, k:k + SP], scalar=par[:, wc + k:wc + k + 1],
                                                                                               in1=ACCD, op0=ALU.mult, op1=ALU.add)),
                              reads=[ek, "par", "ACCD"], writes=["ACCD"])
                    P.add("vector", (lambda e, EXT=EXT, HL=HL: e.tensor_copy(out=HL, in_=EXT[:, SP:SP + 3])), reads=[ek], writes=[hk])
                    P.add("scalar", (lambda e, OUT=OUT, bc=bc: e.activation(out=OUT, in_=ACCD, func=AF.Silu, bias=par[:, bc:bc + 1])),
                          reads=["ACCD", "par"], writes=[ok])
                ld(LDT, "LDT", proj[84 * 128:85 * 128, t0:t0 + SP], ("proj", 84, t0))
                for j in range(SP // 128):
                    cs = slice(j * 128, (j + 1) * 128)
                    P.add("tensor", (lambda e, cs=cs: e.transpose(PT[:, 0:128], LDT[:, cs], ident_b)), reads=["LDT", "cb16"], writes=[("PT", 0)])
                    P.add("vector", (lambda e, h0=h0: e.tensor_tensor(out=DT2, in0=PT[:, h0:h0 + 2], in1=DTB[:, h0:h0 + 2], op=ALU.add)),
                          reads=[("PT", 0), "DTB"], writes=["DT2"])
                    P.add("scalar", lambda e: e.activation(out=DT2, in_=DT2, func=AF.Exp), reads=["DT2"], writes=["DT2"])
                    P.add("scalar", lambda e: e.activation(out=DT2, in_=DT2, func=AF.Ln, bias=1.0), reads=["DT2"], writes=["DT2"])
                    P.add("vector", (lambda e, h0=h0: e.tensor_tensor(out=DA, in0=DT2, in1=A2[:, h0:h0 + 2], op=ALU.mult)), reads=["DT2", "A2"], writes=["DA"])
                    for h in range(2):
                        P.add("vector", (lambda e, h=h: e.tensor_scalar(out=DAB[h], in0=ones_f[:], scalar1=DA[:, h:h + 1], scalar2=None, op0=ALU.mult)),
                              reads=["DA", "ones_f"], writes=[("DAB", h)])
                        P.add("tensor", (lambda e, h=h: e.matmul(PSB[h][:, 0:128], lhsT=DAB[h], rhs=LE_f, start=True, stop=True)),
                              reads=[("DAB", h), "cf32"], writes=[("ps", h)])
                    P.add("tensor", lambda e: e.matmul(PSB[2][:, 0:2], lhsT=LE_f, rhs=DA, start=True, stop=True), reads=["DA", "cf32"], writes=[("ps", 2)])
                    P.add("vector", lambda e: e.tensor_copy(out=CUMC, in_=PSB[2][:, 0:2]), reads=[("ps", 2)], writes=["CUMC"])
                    for h in range(2):
                        P.add("vector", (lambda e, h=h: e.tensor_copy(out=CL[:, h:h + 1], in_=PSB[h][:, 127:128])), reads=[("ps", h)], writes=["CL"])
                    P.add("tensor", (lambda e, cs=cs: e.matmul(PSB[3][:, 0:128], lhsT=BC[:, cs], rhs=CC[:, cs], start=True, stop=True)),
                          reads=["BC", "CC"], writes=[("ps", 3)])
                    P.add("tensor", (lambda e, cs=cs: e.transpose(PT[:, 128:256], XC[:, cs], ident_b)), reads=["XC", "cb16"], writes=[("PT", 1)])
                    P.add("vector", lambda e: e.tensor_scalar(out=XDTA[:, 0:64], in0=PT[:, 128:192], scalar1=DT2[:, 0:1], scalar2=None, op0=ALU.mult),
                          reads=[("PT", 1), "DT2"], writes=["XDTA"])
                    P.add("vector", lambda e: e.tensor_scalar(out=XDTB[:, 64:128], in0=PT[:, 192:256], scalar1=DT2[:, 1:2], scalar2=None, op0=ALU.mult),
                          reads=[("PT", 1), "DT2"], writes=["XDTB"])
                    for h in range(2):
                        P.add("vector", (lambda e, h=h: e.tensor_scalar(out=TD[h], in0=PSB[h][:, 0:128], scalar1=CUMC[:, h:h + 1], scalar2=0.0,
                                                                        op0=ALU.subtract, op1=ALU.min)),
                              reads=[("ps", h), "CUMC"], writes=[("TD", h)])
                        P.add("scalar", (lambda e, h=h: e.activation(out=TD[h], in_=TD[h], func=AF.Exp)), reads=[("TD", h)], writes=[("TD", h)])
                        P.add("vector", (lambda e, h=h: e.tensor_tensor(out=TD[h], in0=TD[h], in1=LE_f, op=ALU.mult)), reads=[("TD", h), "cf32"], writes=[("TD", h)])
                        P.add("vector", (lambda e, h=h: e.tensor_tensor(out=MH[h], in0=PSB[3][:, 0:128], in1=TD[h], op=ALU.mult)),
                              reads=[("ps", 3), ("TD", h)], writes=[("MH", h)])
                        P.add("scalar", (lambda e, h=h: e.activation(out=ECB[h], in_=PSB[h][:, 0:128], func=AF.Exp)), reads=[("ps", h)], writes=[("ECB", h)])
                        P.add("vector", (lambda e, h=h, cs=cs: e.tensor_tensor(out=CS[h], in0=CC[:, cs], in1=ECB[h], op=ALU.mult)),
                              reads=["CC", ("ECB", h)], writes=[("CS", h)])
                    P.add("tensor", lambda e: e.matmul(PSB[4][:, 0:128], lhsT=XDTA, rhs=MH[0], start=True, stop=False), reads=["XDTA", ("MH", 0)], writes=[("ps", 4)])
                    P.add("tensor", lambda e: e.matmul(PSB[4][:, 0:128], lhsT=XDTB, rhs=MH[1], start=False, stop=False), reads=["XDTB", ("MH", 1)], writes=[("ps", 4)])
                    P.add("tensor", lambda e: e.matmul(PSB[4][:, 0:128], lhsT=HTA, rhs=CS[0], start=False, stop=False), reads=["HTA", ("CS", 0)], writes=[("ps", 4)])
                    P.add("tensor", lambda e: e.matmul(PSB[4][:, 0:128], lhsT=HTB, rhs=CS[1], start=False, stop=True), reads=["HTB", ("CS", 1)], writes=[("ps", 4)])
                    P.add("vector", (lambda e, cs=cs, c=c: e.scalar_tensor_tensor(out=YD[:, cs], in0=XC[:, cs], scalar=par[:, PO["dskip"] + c:PO["dskip"] + c + 1],
                                                                                   in1=PSB[4][:, 0:128], op0=ALU.mult, op1=ALU.add)),
                          reads=["XC", "par", ("ps", 4)], writes=["YD"])
                    for h in range(2):
                        P.add("scalar", (lambda e, h=h: e.activation(out=DE[:, h:h + 1], in_=CUMC[:, h:h + 1], func=AF.Exp, scale=-1.0, bias=CL[:, h:h + 1])),
                              reads=["CUMC", "CL"], writes=["DE"])
                    P.add("scalar", lambda e: e.activation(out=CD, in_=CL, func=AF.Exp), reads=["CL"], writes=["CD"])
                    P.add("vector", lambda e: e.tensor_scalar(out=XDE[:, 0:64], in0=XDTA[:, 0:64], scalar1=DE[:, 0:1], scalar2=None, op0=ALU.mult),
                          reads=["XDTA", "DE"], writes=["XDE"])
                    P.add("vector", lambda e: e.tensor_scalar(out=XDE[:, 64:128], in0=XDTB[:, 64:128], scalar1=DE[:, 1:2], scalar2=None, op0=ALU.mult),
                          reads=["XDTB", "DE"], writes=["XDE"])
                    P.add("tensor", (lambda e, cs=cs: e.transpose(PT[:, 256:384], BC[:, cs], ident_b)), reads=["BC", "cb16"], writes=[("PT", 2)])
                    P.add("vector", lambda e: e.tensor_copy(out=BT, in_=PT[:, 256:384]), reads=[("PT", 2)], writes=["BT"])
                    P.add("tensor", lambda e: e.matmul(PSB[5][:, 0:128], lhsT=BT, rhs=XDE, start=True, stop=True), reads=["BT", "XDE"], writes=[("ps", 5)])
                    P.add("vector", lambda e: e.scalar_tensor_tensor(out=HF[:, 0:64], in0=HF[:, 0:64], scalar=CD[:, 0:1], in1=PSB[5][:, 0:64], op0=ALU.mult, op1=ALU.add),
                          reads=["HF", "CD", ("ps", 5)], writes=["HF"])
                    P.add("vector", lambda e: e.scalar_tensor_tensor(out=HF[:, 64:128], in0=HF[:, 64:128], scalar=CD[:, 1:2], in1=PSB[5][:, 64:128], op0=ALU.mult, op1=ALU.add),
                          reads=["HF", "CD", ("ps", 5)], writes=["HF"])
                    P.add("vector", lambda e: e.tensor_copy(out=HTA[:, 0:64], in_=HF[:, 0:64]), reads=["HF"], writes=["HTA"])
                    P.add("vector", lambda e: e.tensor_copy(out=HTB[:, 64:128], in_=HF[:, 64:128]), reads=["HF"], writes=["HTB"])
                    if dbg and c == 0 and i == 0 and j == 0:
                        items = [(DTB, 48, "DTB"), (A2, 16, "A2"), (DT2, 2, "DT2"), (DA, 2, "DA"), (CUMC, 2, "CUMC"), (CL, 2, "CL"), (DE, 2, "DE"), (CD, 2, "CD"),
                                 (TD[0], 128, ("TD", 0)), (MH[0], 128, ("MH", 0)), (ECB[0], 128, ("ECB", 0)), (CS[0], 128, ("CS", 0)),
                                 (XDTA, 128, "XDTA"), (XC[:, 0:128], 128, "XC"), (BC[:, 0:128], 128, "BC"), (CC[:, 0:128], 128, "CC"), (HF, 128, "HF"),
                                 (YD[:, 0:128], 128, "YD"), (XDE, 128, "XDE"), (BT, 128, "BT")]
                        o = 0
                        for (ap_, w_, k_) in items:
                            P.add("vector", (lambda e, ap_=ap_, o=o, w_=w_: e.tensor_copy(out=DBG[:, o:o + w_], in_=ap_)), reads=[k_], writes=["DBG"])
                            print("DBGITEM", k_, o, w_)
                            o += w_
                        P.add("sync", lambda e: e.dma_start(out=dbg_d[:, :], in_=DBG), reads=["DBG"], writes=["dbg_d"], dma_key="dbg")
                store("sync", ymix[(24 + c) * 128:(25 + c) * 128, t0:t0 + SP], YD, "YD", ("ymix", 24 + c, t0))

    def phase3(l, Wl, src, dst):
        soff[0] = 0
        cast_weights(l, ["w_br", "w_out", "w_g", "w_u", "w_d"])
        YG = [sf(TB) for _ in range(4)]
        ZT = sh(TB); YT = sh(TB); T1 = sf(TB); T2 = sf(TB); R3 = sf(TB)
        macc = sf(TB)
        assert soff[0] * 2 <= 20 * TB
        for tb in range(NTB):
            t0 = tb * TB
            for m in range(3):
                for c in range(8):
                    kk = m * 8 + c
                    P.add("sync", (lambda e, kk=kk, t0=t0: e.dma_start(out=HT[:, kk, :], in_=ymix[kk * 128:(kk + 1) * 128, t0:t0 + TB])),
                          reads=[("ymix", kk, t0)], writes=[("HT", kk)], dma_key=("ldht", kk % 4))
            for g in range(2):
                for q in range(4):
                    c = g * 4 + q
                    ld(YT, "YT", ymix[(24 + c) * 128:(25 + c) * 128, t0:t0 + TB], ("ymix", 24 + c, t0))
                    nb = c * 10 + 8
                    ld(ZT, "ZT", proj[nb * 128:(nb + 1) * 128, t0:t0 + TB], ("proj", nb, t0))
                    P.add("scalar", lambda e: e.activation(out=T1, in_=ZT, func=AF.Silu), reads=["ZT"], writes=["T1"])
                    P.add("vector", (lambda e, q=q: e.tensor_tensor(out=YG[q], in0=YT, in1=T1, op=ALU.mult)), reads=["YT", "T1"], writes=[("YG", q)])
                    P.add("scalar", (lambda e, q=q: e.activation(out=T2, in_=YG[q], func=AF.Square)), reads=[("YG", q)], writes=["T2"])
                    P.add("tensor", (lambda e, q=q: e.matmul(PSB[4][:], lhsT=ones_f[:], rhs=T2, start=(q == 0), stop=(q == 3))),
                          reads=["T2", "ones_f"], writes=[("ps", 4)])
                P.add("vector", lambda e: e.tensor_scalar(out=R3, in0=PSB[4][:], scalar1=1.0 / 512, scalar2=EPS, op0=ALU.mult, op1=ALU.add),
                      reads=[("ps", 4)], writes=["R3"])
                P.add("scalar", lambda e: e.activation(out=R3, in_=R3, func=AF.Sqrt), reads=["R3"], writes=["R3"])
                P.add("vector", lambda e: e.reciprocal(out=R3, in_=R3), reads=["R3"], writes=["R3"])
                for q in range(4):
                    c = g * 4 + q
                    P.add("vector", (lambda e, q=q, c=c: e.scalar_tensor_tensor(out=HT[:, 24 + c, :], in0=YG[q], scalar=par[:, PO["ssm_norm"] + c:PO["ssm_norm"] + c + 1],
                                                                                 in1=R3, op0=ALU.mult, op1=ALU.mult)),
                          reads=[("YG", q), "R3", "par"], writes=[("HT", 24 + c)])
            wb = Wl["w_br_b"]
            for pi in range(D // NP):
                slot = rot("wp", 3); wp = WP[slot]
                P.add("sync", (lambda e, wp=wp, pi=pi: e.dma_start(out=wp[:, 0:KC * NP], in_=wb[pi])),
                      reads=[("w_br_b", l, pi)], writes=[("wp", slot)], dma_key=("wp", slot))
                for j in range(NP // 128):
                    nb = pi * (NP // 128) + j
                    for m in range(4):
                        b = rot("ps", 4); pt = PSB[b]
                        for c in range(8):
                            kk = m * 8 + c
                            P.add("tensor", (lambda e, pt=pt, wp=wp, kk=kk, j=j, c=c: e.matmul(
                                pt[:], lhsT=wp[:, kk * NP + j * 128:kk * NP + (j + 1) * 128], rhs=HT[:, kk, :], start=(c == 0), stop=(c == 7))),
                                reads=[("wp", slot), ("HT", kk)], writes=[("ps", b)])
                        gi = rot("gt", 6)
                        gr = m * 32 + nb
                        P.add("sync", (lambda e, gi=gi, gr=gr, t0=t0: e.dma_start(out=GT[gi][:], in_=gates[gr * 128:(gr + 1) * 128, t0:t0 + TB])),
                              reads=[("gates", gr, t0)], writes=[("gt", gi)], dma_key=("gt", gi))
                        if m == 0:
                            P.add("vector", (lambda e, pt=pt, gi=gi: e.tensor_tensor(out=macc, in0=pt[:], in1=GT[gi][:], op=ALU.mult)),
                                  reads=[("ps", b), ("gt", gi)], writes=["macc"])
                        else:
                            P.add("vector", (lambda e, pt=pt, gi=gi: e.tensor_tensor(out=T1, in0=pt[:], in1=GT[gi][:], op=ALU.mult)),
                                  reads=[("ps", b), ("gt", gi)], writes=["T1"])
                            if m < 3:
                                P.add("vector", lambda e: e.tensor_tensor(out=macc, in0=macc, in1=T1, op=ALU.add), reads=["macc", "T1"], writes=["macc"])
                            else:
                                P.add("vector", (lambda e, nb=nb: e.tensor_tensor(out=BIGb[:, (20 + nb) * TB:(20 + nb + 1) * TB], in0=macc, in1=T1, op=ALU.add)),
                                      reads=["macc", "T1"], writes=[("BIG", 20 + nb)])

            def epi_o(nb, pt, pkey, t0=t0):
                i = rot("stf", 5)
                P.add("sync", (lambda e: e.dma_start(out=STF[i][:], in_=src[nb * 128:(nb + 1) * 128, t0:t0 + TB])), writes=[("stf", i)], dma_key=("stfl", i))
                P.add("vector", (lambda e: e.tensor_tensor(out=STF[i][:], in0=pt[:], in1=STF[i][:], op=ALU.add)), reads=[pkey, ("stf", i)], writes=[("stf", i)])
                store("sync", x1[nb * 128:(nb + 1) * 128, t0:t0 + TB], STF[i][:], ("stf", i), ("x1", nb, t0))
            gemm("w_out", l, (lambda kc: BIGb[:, (20 + kc) * TB:(20 + kc + 1) * TB]), (lambda kc: ("BIG", 20 + kc)), KC, NP, D // NP, epi_o)
            if stop_after == "p3":
                P.barrier()
                continue
            rmsnorm_block(x1, tb, PO["norm_ffn"], xkey=lambda kc, t0=t0: ("x1", kc, t0))
            wg = Wl["w_g_b"]; wu = Wl["w_u_b"]
            for pi in range(DFF // NP):
                sg = rot("wp", 3); su = rot("wp", 3)
                P.add("sync", (lambda e, sg=sg, pi=pi: e.dma_start(out=WP[sg][:, 0:KC * NP], in_=wg[pi])), reads=[("w_g_b", l, pi)], writes=[("wp", sg)], dma_key=("wp", sg))
                P.add("sync", (lambda e, su=su, pi=pi: e.dma_start(out=WP[su][:, 0:KC * NP], in_=wu[pi])), reads=[("w_u_b", l, pi)], writes=[("wp", su)], dma_key=("wp", su))
                for j in range(NP // 128):
                    hb = pi * (NP // 128) + j
                    bg = rot("ps", 4); bu = rot("ps", 4)
                    for (slot, b) in ((sg, bg), (su, bu)):
                        for kc in range(KC):
                            P.add("tensor", (lambda e, slot=slot, b=b, kc=kc, j=j: e.matmul(
                                PSB[b][:], lhsT=WP[slot][:, kc * NP + j * 128:kc * NP + (j + 1) * 128], rhs=HT[:, kc, :], start=(kc == 0), stop=(kc == KC - 1))),
                                reads=[("wp", slot), "HT"], writes=[("ps", b)])
                    i = rot("stf", 5)
                    P.add("scalar", (lambda e, i=i, bg=bg: e.activation(out=STF[i][:], in_=PSB[bg][:], func=AF.Silu)), reads=[("ps", bg)], writes=[("stf", i)])
                    P.add("vector", (lambda e, i=i, bu=bu, hb=hb: e.tensor_tensor(out=BIGb[:, hb * TB:(hb + 1) * TB], in0=PSB[bu][:], in1=STF[i][:], op=ALU.mult)),
                          reads=[("ps", bu), ("stf", i)], writes=[("BIG", hb)])

            def epi_d(nb, pt, pkey, t0=t0):
                i = rot("stf", 5)
                P.add("sync", (lambda e: e.dma_start(out=STF[i][0:64, :], in_=x1[nb * 64:(nb + 1) * 64, t0:t0 + TB])),
                      reads=[("x1", nb // 2, t0)], writes=[("stf", i)], dma_key=("stfl", i))
                P.add("vector", (lambda e: e.tensor_tensor(out=STF[i][0:64, :], in0=pt[0:64, :], in1=STF[i][0:64, :], op=ALU.add)), reads=[pkey, ("stf", i)], writes=[("stf", i)])
                store("sync", dst[nb * 64:(nb + 1) * 64, t0:t0 + TB], STF[i][0:64, :], ("stf", i), ("dst", nb, t0))
            gemm("w_d", l, (lambda kc: BIGb[:, kc * TB:(kc + 1) * TB]), (lambda kc: ("BIG", kc)), KF, 64, D // 64, epi_d)
            P.barrier()

    final = []
    for l in range(depth):
        src = xin if l == 0 else x2
        dst = yout if l == depth - 1 else x2
        Wl = W[l]
        P.add("sync", lambda e, Wl=Wl: e.dma_start(out=par[:], in_=Wl["par"][:, :]), writes=["par"], dma_key="par")
        cast_weights(l, ["w_in"])

        for tb in range(NTB):
            t0 = tb * TB
            rmsnorm_block(src, tb, PO["norm_mix"])

            def epi(nb, pt, pkey, t0=t0):
                i = rot("stg", 4)
                if nb < NB_MIX:
                    P.add("vector", (lambda e: e.tensor_copy(out=STG[i][:], in_=pt[:])), reads=[pkey], writes=[("stg", i)])
                    store("sync", proj[nb * 128:(nb + 1) * 128, t0:t0 + TB], STG[i][:], ("stg", i), ("proj", nb, t0))
                else:
                    g = nb - NB_MIX
                    P.add("scalar", (lambda e: e.activation(out=STG[i][:], in_=pt[:], func=AF.Sigmoid)), reads=[pkey], writes=[("stg", i)])
                    store("sync", gates[g * 128:(g + 1) * 128, t0:t0 + TB], STG[i][:], ("stg", i), ("gates", g, t0))
            gemm("w_in", l, (lambda kc: HT[:, kc, :]), "HT", KC, NP, NPI, epi)
        if stop_after == "p1":
            break
        P.barrier()
        SP = TB
        NSP = S // SP
        V = mybir

        def gelu(X, G, T):
            return [("vector", lambda e: e.tensor_tensor(out=T, in0=X, in1=X, op=ALU.mult)),
                    ("vector", lambda e: e.tensor_scalar(out=T, in0=T, scalar1=0.044715, scalar2=1.0, op0=ALU.mult, op1=ALU.add)),
                    ("vector", lambda e: e.tensor_tensor(out=T, in0=T, in1=X, op=ALU.mult)),
                    ("scalar", lambda e: e.activation(out=T, in_=T, func=AF.Sigmoid, scale=1.5957691216057308)),
                    ("vector", lambda e: e.tensor_tensor(out=G, in0=X, in1=T, op=ALU.mult))]

        def ld(dst, dkey, src_ap, skey, q="sync"):
            return P.add(q, (lambda e: e.dma_start(out=dst, in_=src_ap)), reads=[skey], writes=[dkey], dma_key=("ld", dkey))

        soff[0] = 0
        GV = [sf(TB) for _ in range(8)]
        VB = [sh(TB) for _ in range(8)]
        TMP = sf(TB); MEAN = sf(TB); RST = sf(TB)
        for tb in range(NTB):
            t0 = tb * TB
            for c in range(8):
                nb = c * 10 + 4
                ld(VB[c], ("VB", c), proj[nb * 128:(nb + 1) * 128, t0:t0 + TB], ("proj", nb, t0))
                for (en, fn) in gelu(VB[c], GV[c], TMP):
                    P.add(en, fn, reads=[("VB", c), ("GV", c), "TMP"], writes=[("GV", c), "TMP"])
                P.add("tensor", (lambda e, c=c: e.matmul(PSB[4][:], lhsT=ones_f[:], rhs=GV[c], start=(c == 0), stop=(c == 7))),
                      reads=[("GV", c), "ones_f"], writes=[("ps", 4)])
                P.add("scalar", (lambda e, c=c: e.activation(out=TMP, in_=GV[c], func=AF.Square)), reads=[("GV", c)], writes=["TMP"])
                P.add("tensor", (lambda e, c=c: e.matmul(PSB[5][:], lhsT=ones_f[:], rhs=TMP, start=(c == 0), stop=(c == 7))),
                      reads=["TMP", "ones_f"], writes=[("ps", 5)])
            P.add("vector", lambda e: e.tensor_scalar(out=MEAN, in0=PSB[4][:], scalar1=1.0 / WM, scalar2=None, op0=ALU.mult), reads=[("ps", 4)], writes=["MEAN"])
            P.add("vector", lambda e: e.tensor_tensor(out=TMP, in0=MEAN, in1=MEAN, op=ALU.mult), reads=["MEAN"], writes=["TMP"])
            P.add("vector", lambda e: e.scalar_tensor_tensor(out=RST, in0=PSB[5][:], scalar=1.0 / WM, in1=TMP, op0=ALU.mult, op1=ALU.subtract),
                  reads=[("ps", 5), "TMP"], writes=["RST"])
            P.add("vector", lambda e: e.tensor_scalar(out=RST, in0=RST, scalar1=EPS, scalar2=None, op0=ALU.add), reads=["RST"], writes=["RST"])
            P.add("scalar", lambda e: e.activation(out=RST, in_=RST, func=AF.Sqrt), reads=["RST"], writes=["RST"])
            P.add("vector", lambda e: e.reciprocal(out=RST, in_=RST), reads=["RST"], writes=["RST"])
            for c in range(8):
                nb = c * 10 + 4
                P.add("vector", (lambda e, c=c: e.tensor_tensor(out=GV[c], in0=GV[c], in1=MEAN, op=ALU.subtract)), reads=[("GV", c), "MEAN"], writes=[("GV", c)])
                P.add("vector", (lambda e, c=c: e.scalar_tensor_tensor(out=VB[c], in0=GV[c], scalar=par[:, PO["sgu_norm"] + c:PO["sgu_norm"] + c + 1],
                                                                        in1=RST, op0=ALU.mult, op1=ALU.mult)),
                      reads=[("GV", c), "RST", "par"], writes=[("VB", c)])
                P.add("sync", (lambda e, c=c, nb=nb, t0=t0: e.dma_start(out=proj[nb * 128:(nb + 1) * 128, t0:t0 + TB], in_=VB[c])),
                      reads=[("VB", c)], writes=[("proj", nb, t0)], dma_key=("st", ("VB", c)))
        P.barrier()
        if stop_after == "p1b":
            break

        soff[0] = 0
        WBT = sh(8 * 128)
        BHI = sh(1024); BLO = sh(1024)
        BF = sf(1024); BF2 = sf(1024)
        LB = sh(SP); LC = sh(SP); LX = sh(SP); LU = sh(SP); LV = sh(SP)
        CXE = sf(SP + 2); HAL = sf(2); ACC = sf(SP); YA = sh(SP)
        GU = sf(SP); TG = sf(SP); VT_ = sh(128); YB = sh(SP)
        for c in range(8):
            P.add("sync", (lambda e, c=c, Wl=Wl: e.dma_start(out=TG[:, 0:128], in_=Wl["sguw"][c])), writes=["TG"], dma_key="sguw")
            P.add("vector", (lambda e, c=c: e.tensor_tensor(out=WBT[:, c * 128:(c + 1) * 128], in0=TG[:, 0:128], in1=LE_f, op=ALU.mult)),
                  reads=["TG", "cf32"], writes=["WBT"])
        P.add("sync", lambda e, Wl=Wl: e.dma_start(out=BF[0:1, :], in_=Wl["sgub"][:, :]), writes=["BF"], dma_key="sgub")
        P.add("vector", lambda e: e.tensor_copy(out=BHI[0:1, :], in_=BF[0:1, :]), reads=["BF"], writes=["BHI"])
        P.add("vector", lambda e: e.tensor_copy(out=BF2[0:1, :], in_=BHI[0:1, :]), reads=["BHI"], writes=["BF2"])
        P.add("vector", lambda e: e.tensor_tensor(out=BF2[0:1, :], in0=BF[0:1, :], in1=BF2[0:1, :], op=ALU.subtract), reads=["BF", "BF2"], writes=["BF2"])
        P.add("vector", lambda e: e.tensor_copy(out=BLO[0:1, :], in_=BF2[0:1, :]), reads=["BF2"], writes=["BLO"])
        for c in range(8):
            ca = PO["conv_a"] + 3 * c
            for i in range(NSP):
                t0 = i * SP
                for (tl, k, j) in ((LB, "LB", 0), (LC, "LC", 1), (LX, "LX", 2)):
                    nb = c * 10 + j
                    ld(tl, k, proj[nb * 128:(nb + 1) * 128, t0:t0 + SP], ("proj", nb, t0))
                if i == 0:
                    P.add("vector", lambda e: e.memset(CXE[:, 0:2], 0.0), writes=["CXE"])
                else:
                    P.add("vector", lambda e: e.tensor_copy(out=CXE[:, 0:2], in_=HAL), reads=["HAL"], writes=["CXE"])
                P.add("vector", lambda e: e.tensor_tensor(out=CXE[:, 2:2 + SP], in0=LC, in1=LX, op=ALU.mult), reads=["LC", "LX"], writes=["CXE"])
                P.add("vector", (lambda e, ca=ca: e.tensor_scalar(out=ACC, in0=CXE[:, 0:SP], scalar1=par[:, ca:ca + 1], scalar2=None, op0=ALU.mult)),
                      reads=["CXE", "par"], writes=["ACC"])
                for k in (1, 2):
                    P.add("vector", (lambda e, ca=ca, k=k: e.scalar_tensor_tensor(out=ACC, in0=CXE[:, k:k + SP], scalar=par[:, ca + k:ca + k + 1],
                                                                                   in1=ACC, op0=ALU.mult, op1=ALU.add)),
                          reads=["CXE", "par", "ACC"], writes=["ACC"])
                P.add("vector", lambda e: e.tensor_copy(out=HAL, in_=CXE[:, SP:SP + 2]), reads=["CXE"], writes=["HAL"])
                P.add("vector", lambda e: e.tensor_tensor(out=YA, in0=ACC, in1=LB, op=ALU.mult), reads=["ACC", "LB"], writes=["YA"])
                store("sync", ymix[(0 * 8 + c) * 128:(0 * 8 + c + 1) * 128, t0:t0 + SP], YA, "YA", ("ymix", c, t0))
                ld(LU, "LU", proj[(c * 10 + 3) * 128:(c * 10 + 4) * 128, t0:t0 + SP], ("proj", c * 10 + 3, t0))
                ld(LV, "LV", proj[(c * 10 + 4) * 128:(c * 10 + 5) * 128, t0:t0 + SP], ("proj", c * 10 + 4, t0))
                for (en, fn) in gelu(LU, GU, TG):
                    P.add(en, fn, reads=["LU", "GU", "TG"], writes=["GU", "TG"])
                for j in range(SP // 128):
                    P.add("tensor", (lambda e, j=j: e.transpose(PT[:, 0:128], LV[:, j * 128:(j + 1) * 128], ident_b)),
                          reads=["LV", "cb16"], writes=["PT"])
                    P.add("vector", lambda e: e.tensor_copy(out=VT_, in_=PT[:, 0:128]), reads=["PT"], writes=["VT_"])
                    P.add("tensor", (lambda e, c=c: e.matmul(PSB[4][:, 0:128], lhsT=VT_, rhs=WBT[:, c * 128:(c + 1) * 128], start=True, stop=False)),
                          reads=["VT_", "WBT"], writes=[("ps", 4)])
                    P.add("tensor", (lambda e, c=c: e.matmul(PSB[4][:, 0:128], lhsT=ones_b[0:1, :], rhs=BHI[0:1, c * 128:(c + 1) * 128], start=False, stop=False)),
                          reads=["BHI", "ones_b"], writes=[("ps", 4)])
                    P.add("tensor", (lambda e, c=c: e.matmul(PSB[4][:, 0:128], lhsT=ones_b[0:1, :], rhs=BLO[0:1, c * 128:(c + 1) * 128], start=False, stop=True)),
                          reads=["BLO", "ones_b"], writes=[("ps", 4)])
                    P.add("vector", (lambda e, j=j: e.tensor_tensor(out=YB[:, j * 128:(j + 1) * 128], in0=PSB[4][:, 0:128], in1=GU[:, j * 128:(j + 1) * 128], op=ALU.mult)),
                          reads=[("ps", 4), "GU"], writes=["YB"])
                store("sync", ymix[(1 * 8 + c) * 128:(1 * 8 + c + 1) * 128, t0:t0 + SP], YB, "YB", ("ymix", 8 + c, t0))
        P.barrier()
        mix_c(l, Wl)
        P.barrier()
        mix_d(l, Wl)
        P.barrier()
        if stop_after == "p2":
            break
        phase3(l, Wl, src, dst)
        P.barrier()
    fw = [op for k, op in P.last_w.items() if op.is_dma]
    with nc.Block() as block:
        P.emit(block, fw)
    return nc, es


def prep_weights(l, inp):
    perm = in_col_perm()
    w_in = inp["w_in"][l]
    wp = np.zeros((D, len(perm)), np.float32)
    m = perm >= 0
    wp[:, m] = w_in[:, perm[m]]
    d = {"w_in%d" % l: panelize(wp, NP)}
    del wp
    if "w_branch" in inp:
        d["w_br%d" % l] = panelize(inp["w_branch"][l].reshape(D, D), NP)
        d["w_out%d" % l] = panelize(inp["w_out"][l], NP)
    if "w_ffn_gate" in inp:
        d["w_g%d" % l] = panelize(inp["w_ffn_gate"][l], NP)
        d["w_u%d" % l] = panelize(inp["w_ffn_up"][l], NP)
        d["w_d%d" % l] = panelize(inp["w_ffn_down"][l], 64)
    d.update(small_inputs(l, inp))
    return d


def kernel(**inputs):
    inp = {k: np.asarray(v) for k, v in inputs.items()}
    x = inp["x"]
    Bsz, S, _ = x.shape
    depth = inp["w_in"].shape[0]
    DFF = inp["w_ffn_gate"].shape[2]
    nc, es = build_nc(S, DFF, depth)
    shared = {"consts": build_consts()}
    for l in range(depth):
        shared.update(prep_weights(l, inp))
    in_maps = []
    for b in range(Bsz):
        d = dict(shared)
        d["xT"] = np.ascontiguousarray(x[b].T)
        in_maps.append(d)
    res = run_bass_kernel_spmd(nc, in_maps, core_ids=list(range(Bsz)))
    out = np.stack([np.ascontiguousarray(res.results[b]["yT"].T) for b in range(Bsz)], axis=0)
    return out.astype(np.float32)
```

```python
import math
from contextlib import ExitStack
import numpy as np
import concourse.bass as bass
import concourse.mybir as mybir
from concourse.bass_utils import run_bass_kernel_spmd

F32 = mybir.dt.float32
BF16 = mybir.dt.bfloat16
ALU = mybir.AluOpType
AF = mybir.ActivationFunctionType

D = 4096
KC = D // 128
WM = 1024
EPS = 1e-6
NB_MIX = 86
NB_GATE = 128
NP = 256
SAME_ENG_SYNC = True


class Op:
    __slots__ = ("eng", "fn", "deps", "needed", "mark", "sem", "is_dma", "idx")

    def __init__(self, eng, fn, is_dma):
        self.eng = eng; self.fn = fn; self.deps = []; self.needed = False
        self.mark = None; self.sem = None; self.is_dma = is_dma


class Plan:
    ENGS = ("sync", "tensor", "vector", "scalar", "gpsimd")

    def __init__(self, nc, es):
        self.nc = nc; self.es = es
        self.ops = {e: [] for e in self.ENGS}
        self.last_w = {}; self.readers = {}
        self.esem = {e: es.enter_context(nc.semaphore("es_" + e)) for e in self.ENGS}
        self.dsem = {}; self.dcount = {}
        self.pend = {e: [] for e in self.ENGS}; self.dmas = []

    def barrier(self):
        deps = [o for e in self.ENGS for o in reversed(self.ops[e]) if not o.is_dma][:0]
        for e in self.ENGS:
            for o in reversed(self.ops[e]):
                if not o.is_dma:
                    deps.append(o); break
        deps = deps + self.dmas
        self.dmas = []
        for e in self.ENGS:
            self.pend[e] = list(deps)

    def add(self, eng, fn, reads=(), writes=(), dma_key=None):
        op = Op(eng, fn, dma_key is not None)
        deps = list(self.pend[eng]); self.pend[eng] = []
        for k in reads:
            w = self.last_w.get(k)
            if w is not None: deps.append(w)
        for k in writes:
            w = self.last_w.get(k)
            if w is not None: deps.append(w)
            deps.extend(self.readers.get(k, ()))
        best = {}
        for d in deps:
            if d is op: continue
            if (not d.is_dma) and d.eng == eng and not op.is_dma and (eng == "tensor" or not SAME_ENG_SYNC):
                continue
            key = ("d", id(d.sem)) if d.is_dma else ("e", d.eng)
            rank = d.mark if d.is_dma else d.idx
            if key not in best or rank > best[key][0]:
                best[key] = (rank, d)
        for (_, d) in best.values():
            d.needed = True
            op.deps.append(d)
        for k in reads:
            self.readers.setdefault(k, []).append(op)
        for k in writes:
            self.last_w[k] = op; self.readers[k] = []
        if op.is_dma:
            if dma_key not in self.dsem:
                self.dsem[dma_key] = self.es.enter_context(self.nc.semaphore("ds%d" % len(self.dsem)))
                self.dcount[dma_key] = 0
            self.dcount[dma_key] += 16
            op.sem = self.dsem[dma_key]; op.mark = self.dcount[dma_key]; op.needed = True
            self.dmas.append(op)
        op.idx = len(self.ops[eng])
        self.ops[eng].append(op)
        return op

    def emit(self, block, final_waits):
        for e in self.ENGS:
            c = 0
            for op in self.ops[e]:
                if not op.is_dma and op.needed:
                    c += 1; op.mark = c; op.sem = self.esem[e]
        plan = self

        def run(engname):
            def body(eng):
                waited = {}
                for op in plan.ops[engname]:
                    for d in op.deps:
                        key = id(d.sem)
                        if waited.get(key, 0) >= d.mark: continue
                        waited[key] = d.mark
                        eng.wait_ge(d.sem, d.mark)
                    ins = op.fn(eng)
                    if op.is_dma:
                        ins.then_inc(op.sem, 16)
                    elif op.needed:
                        ins.then_inc(op.sem, 1)
                if engname == "sync":
                    for d in final_waits:
                        eng.wait_ge(d.sem, d.mark)
            return body
        block.sync(run("sync")); block.tensor(run("tensor")); block.vector(run("vector"))
        block.scalar(run("scalar")); block.gpsimd(run("gpsimd"))


def in_col_perm():
    o = {}
    names = ["a_b", "a_c", "a_x", "b_u", "b_v", "c_q", "c_k", "c_v", "d_z"]
    off = 0
    for n in names:
        o[n] = off; off += WM
    o["d_xs"] = off; off += WM
    o["d_B"] = off; off += 256
    o["d_C"] = off; off += 256
    o["d_dt"] = off; off += 16
    o["gates"] = off
    cols = []
    for c in range(8):
        for n in ["a_b", "a_c", "a_x", "b_u", "b_v", "c_q", "c_k", "c_v", "d_z", "d_xs"]:
            cols.extend(range(o[n] + c * 128, o[n] + (c + 1) * 128))
    for g in range(2):
        cols.extend(range(o["d_B"] + g * 128, o["d_B"] + (g + 1) * 128))
        cols.extend(range(o["d_C"] + g * 128, o["d_C"] + (g + 1) * 128))
    cols.extend(range(o["d_dt"], o["d_dt"] + 16)); cols.extend([-1] * 112)
    cols.extend([-1] * 128)
    assert len(cols) == NB_MIX * 128
    cols.extend(range(o["gates"], o["gates"] + 4 * D))
    return np.array(cols, dtype=np.int64)


def panelize(w, npw):
    K, N = w.shape
    assert K % 128 == 0 and N % npw == 0
    a = w.reshape(K // 128, 128, N // npw, npw)
    return np.ascontiguousarray(a.transpose(2, 1, 0, 3)).reshape(N // npw, 128, (K // 128) * npw)


def vec_cols(v):
    return np.ascontiguousarray(v.reshape(-1, 128).T)


PO = {}
def _po():
    off = 0
    for name, n in [("norm_mix", 32), ("norm_ffn", 32), ("conv_a", 24), ("sgu_norm", 8), ("qn", 1), ("kn", 1),
                    ("cw_x", 32), ("cb_x", 8), ("cw_bc", 16), ("cb_bc", 4), ("ssm_norm", 8), ("dskip", 8)]:
        PO[name] = off; off += n
    PO["_n"] = off
_po()


def build_params(l, inp):
    p = np.zeros((128, PO["_n"]), np.float32)
    p[:, PO["norm_mix"]:PO["norm_mix"] + 32] = vec_cols(inp["norm_mix"][l])
    p[:, PO["norm_ffn"]:PO["norm_ffn"] + 32] = vec_cols(inp["norm_ffn"][l])
    ca = inp["conv_a"][l]
    for c in range(8):
        p[:, PO["conv_a"] + 3 * c:PO["conv_a"] + 3 * c + 3] = ca[:, c * 128:(c + 1) * 128].T
    p[:, PO["sgu_norm"]:PO["sgu_norm"] + 8] = vec_cols(inp["sgu_norm"][l])
    p[:, PO["qn"]] = inp["q_norm"][l]; p[:, PO["kn"]] = inp["k_norm"][l]
    cw = inp["ssm_conv_w"][l]; cb = inp["ssm_conv_b"][l]
    for c in range(8):
        p[:, PO["cw_x"] + 4 * c:PO["cw_x"] + 4 * c + 4] = cw[:, c * 128:(c + 1) * 128].T
        p[:, PO["cb_x"] + c] = cb[c * 128:(c + 1) * 128]
    for g in range(2):
        for j in range(2):
            o = 1024 + j * 256 + g * 128
            i = g * 2 + j
            p[:, PO["cw_bc"] + 4 * i:PO["cw_bc"] + 4 * i + 4] = cw[:, o:o + 128].T
            p[:, PO["cb_bc"] + i] = cb[o:o + 128]
    p[:, PO["ssm_norm"]:PO["ssm_norm"] + 8] = vec_cols(inp["ssm_norm"][l])
    p[:, PO["dskip"]:PO["dskip"] + 8] = np.repeat(inp["ssm_d"][l].reshape(8, 2), 64, axis=1).T
    return p


def small_inputs(l, inp):
    return {"par%d" % l: build_params(l, inp),
            "dtb%d" % l: np.concatenate([inp["ssm_dt_bias"][l], inp["ssm_a_log"][l], inp["ssm_d"][l]])[None, :].astype(np.float32),
            "sguw%d" % l: np.ascontiguousarray(inp["sgu_w"][l].transpose(0, 2, 1)),
            "sgub%d" % l: np.ascontiguousarray(inp["sgu_b"][l].reshape(1, 1024))}


def build_consts():
    s = np.arange(128)[:, None]
    t = np.arange(512)[None, :]
    c = np.zeros((128, 4 * 512 + 4 * 128), np.float32)
    for r in range(4):
        c[:, r * 512:(r + 1) * 512] = ((r * 128 + s) < t)
    t1 = np.arange(128)[None, :]
    o = 4 * 512
    c[:, o:o + 128] = (s <= t1)
    c[:, o + 128:o + 256] = -1.0 * (s >= t1)
    c[:, o + 256:o + 384] = -1.0
    c[:, o + 384:o + 512] = np.eye(128)
    return c


def build_nc(S, DFF, depth, stop_after=None, dbg=False):
    TB = 512
    NTB = S // TB
    KF = DFF // 128
    nc = bass.Bass("TRN2", target_bir_lowering=False)
    es = ExitStack()
    dr = lambda name, shape, dt, kind="Internal": nc.dram_tensor(name, shape, dt, kind=kind).ap()
    xin = dr("xT", [D, S], F32, "ExternalInput")
    yout = dr("yT", [D, S], F32, "ExternalOutput")
    consts_d = dr("consts", [128, 4 * 512 + 512], F32, "ExternalInput")
    NPI = (NB_MIX + NB_GATE) // 2
    W = []
    need = {"p1": ("w_in",), "p2": ("w_in",), "p3": ("w_in", "w_br", "w_out")}.get(stop_after, ("w_in", "w_br", "w_out", "w_g", "w_u", "w_d"))
    ikind = "ExternalOutput" if dbg else "Internal"
    for l in range(depth):
        W.append(dict(
            par=dr("par%d" % l, [128, PO["_n"]], F32, "ExternalInput"),
            dtb=dr("dtb%d" % l, [1, 48], F32, "ExternalInput"),
            sguw=dr("sguw%d" % l, [8, 128, 128], F32, "ExternalInput"),
            sgub=dr("sgub%d" % l, [1, 1024], F32, "ExternalInput"),
        ))
        wshapes = dict(w_in=[NPI, 128, KC * NP], w_br=[D // NP, 128, KC * NP], w_out=[D // NP, 128, KC * NP],
                       w_g=[DFF // NP, 128, KC * NP], w_u=[DFF // NP, 128, KC * NP], w_d=[D // 64, 128, KF * 64])
        for k in need:
            W[l][k] = dr("%s%d" % (k, l), wshapes[k], F32, "ExternalInput")
            W[l][k + "_b"] = dr("%s%d_b" % (k, l), wshapes[k], BF16)
    x1 = dr("x1", [D, S], F32)
    x2 = dr("x2", [D, S], F32)
    proj = dr("proj", [NB_MIX * 128, S], BF16, ikind)
    gates = dr("gates", [NB_GATE * 128, S], BF16, ikind)
    ymix = dr("ymix", [32 * 128, S], BF16, ikind)
    dbg_d = dr("dbg", [128, 2048], F32, "ExternalOutput") if dbg else None

    sb = lambda name, shape, dt: es.enter_context(nc.sbuf_tensor(name, shape, dt))
    ps = lambda name, shape, dt=F32: es.enter_context(nc.psum_tensor(name, shape, dt))
    P = Plan(nc, es)

    cb16 = sb("consts_b", [128, 4 * 512 + 512], BF16)
    cf32 = sb("consts_f", [128, 512], F32)
    ones_f = sb("ones_f", [128, 128], F32)
    ones_b = sb("ones_b", [128, 128], BF16)
    par = sb("par_s", [128, PO["_n"]], F32)
    CO = 4 * 512
    maskLT = lambda r: cb16[:, r * 512:(r + 1) * 512]
    LE_f = cf32[:, 0:128]
    LE_b = cb16[:, CO:CO + 128]
    NU_b = cb16[:, CO + 128:CO + 256]
    NEG1_b = cb16[:, CO + 256:CO + 384]
    ident_b = cb16[:, CO + 384:CO + 512]
    ident_f = cf32[:, 384:512]

    HT = sb("HT", [128, KC, TB], BF16)
    WPN = KC * NP
    WP = [sb("WP%d" % i, [128, WPN], BF16) for i in range(3)]
    BIGF = max(KF, 56) * TB // 2
    BIG = sb("BIG", [128, BIGF], F32)
    BIGb = BIG[:, :].bitcast(BF16)
    STG = [sb("STG%d" % i, [128, TB], BF16) for i in range(4)]
    GT = [sb("GT%d" % i, [128, TB], BF16) for i in range(6)]
    STF = [sb("STF%d" % i, [128, TB], F32) for i in range(5)]
    RS = sb("RS", [128, TB], F32)
    PSB = [ps("PS%d" % i, [128, TB]) for i in range(7)]
    PT = ps("PT", [128, 1024], BF16)
    soff = [0]

    def sf(n):
        a = BIG[:, soff[0]:soff[0] + n]; soff[0] += n; assert soff[0] <= BIGF; return a

    def sh(n):
        m = (n + 1) // 2
        a = BIG[:, soff[0]:soff[0] + m].bitcast(BF16); soff[0] += m; assert soff[0] <= BIGF; return a

    cnt = {"wp": 0, "ps": 0, "stg": 0, "stf": 0, "gt": 0}

    def rot(name, n):
        i = cnt[name] % n; cnt[name] += 1; return i

    for q in range(5):
        P.add("sync", (lambda e, q=q: e.dma_start(out=STF[q][:], in_=consts_d[:, q * 512:(q + 1) * 512])), writes=[("stf", q)], dma_key=("stfl", q))
        P.add("vector", (lambda e, q=q: e.tensor_copy(out=cb16[:, q * 512:(q + 1) * 512], in_=STF[q][:])), reads=[("stf", q)], writes=["cb16"])
    P.add("vector", lambda e: e.tensor_copy(out=cf32[:], in_=STF[4][:]), reads=[("stf", 4)], writes=["cf32"])
    P.add("vector", lambda e: e.memset(ones_f[:], 1.0), writes=["ones_f"])
    P.add("vector", lambda e: e.memset(ones_b[:], 1.0), writes=["ones_b"])

    def cast_weights(l, names):
        for k in names:
            src = W[l][k]; dst = W[l][k + "_b"]
            for pi in range(src.shape[0]):
                P.add("gpsimd", (lambda e, s=src, d=dst, pi=pi: e.dma_start(out=d[pi], in_=s[pi])),
                      writes=[(k + "_b", l, pi)], dma_key=("cast", pi % 8))

    def gemm(wname, l, act, act_key, kc_n, npw, n_panels, epilogue, panel_ok=None):
        wb = W[l][wname + "_b"]
        per = max(npw // 128, 1)
        for pi in range(n_panels):
            if panel_ok is not None and not panel_ok(pi): continue
            slot = rot("wp", 3)
            wp = WP[slot]
            P.add("sync", (lambda e, wp=wp, pi=pi: e.dma_start(out=wp[:, 0:kc_n * npw], in_=wb[pi])),
                  reads=[(wname + "_b", l, pi)], writes=[("wp", slot)], dma_key=("wp", slot))
            for j in range(per):
                b = rot("ps", 4)
                pt = PSB[b]
                for kc in range(kc_n):
                    P.add("tensor", (lambda e, pt=pt, wp=wp, kc=kc, j=j: e.matmul(
                        pt[0:min(npw, 128), :], lhsT=wp[:, kc * npw + j * 128: kc * npw + j * 128 + min(npw, 128)], rhs=act(kc),
                        start=(kc == 0), stop=(kc == kc_n - 1))),
                        reads=[("wp", slot), act_key(kc) if callable(act_key) else act_key], writes=[("ps", b)])
                epilogue(pi * per + j, pt, ("ps", b))

    def store(eng_q, dst_ap, src_tile, src_key, dst_key):
        return P.add(eng_q, (lambda e: e.dma_start(out=dst_ap, in_=src_tile)), reads=[src_key], writes=[dst_key],
                     dma_key=("st", src_key))

    def rmsnorm_block(src, tb, gcol, xkey=None):
        t0 = tb * TB
        pt = PSB[4]
        for kc in range(KC):
            i = rot("stf", 5)
            P.add("gpsimd", (lambda e, i=i, kc=kc: e.dma_start(out=STF[i][:], in_=src[kc * 128:(kc + 1) * 128, t0:t0 + TB])),
                  reads=([xkey(kc)] if xkey else []), writes=[("stf", i)], dma_key=("stfl", i))
            P.add("scalar", (lambda e, i=i: e.activation(out=STF[i][:], in_=STF[i][:], func=AF.Square)),
                  reads=[("stf", i)], writes=[("stf", i)])
            P.add("tensor", (lambda e, i=i, kc=kc: e.matmul(pt[:], lhsT=ones_f[:], rhs=STF[i][:], start=(kc == 0), stop=(kc == KC - 1))),
                  reads=[("stf", i), "ones_f"], writes=[("ps", 4)])
        P.add("vector", lambda e: e.tensor_scalar(out=RS[:], in0=pt[:], scalar1=1.0 / D, scalar2=EPS, op0=ALU.mult, op1=ALU.add),
              reads=[("ps", 4)], writes=["RS"])
        P.add("scalar", lambda e: e.activation(out=RS[:], in_=RS[:], func=AF.Sqrt), reads=["RS"], writes=["RS"])
        P.add("vector", lambda e: e.reciprocal(out=RS[:], in_=RS[:]), reads=["RS"], writes=["RS"])
        for kc in range(KC):
            i = rot("stf", 5)
            P.add("gpsimd", (lambda e, i=i, kc=kc: e.dma_start(out=STF[i][:], in_=src[kc * 128:(kc + 1) * 128, t0:t0 + TB])),
                  reads=([xkey(kc)] if xkey else []), writes=[("stf", i)], dma_key=("stfl", i))
            P.add("vector", (lambda e, kc=kc, i=i: e.scalar_tensor_tensor(
                out=HT[:, kc, :], in0=STF[i][:], scalar=par[:, gcol + kc:gcol + kc + 1], in1=RS[:], op0=ALU.mult, op1=ALU.mult)),
                reads=[("stf", i), "RS", "par"], writes=["HT"])

    SP = TB
    NSP = S // SP

    def ld(dst, dkey, src_ap, skey, q="gpsimd"):
        return P.add(q, (lambda e: e.dma_start(out=dst, in_=src_ap)), reads=[skey], writes=[dkey], dma_key=("ld", dkey))

    def mix_c(l, Wl):
        soff[0] = 0
        QN = sh(S); KN = sh(S); VT = sh(S)
        LQ = sh(SP); SQ = sf(SP); RQ = sf(SP)
        E = [sf(SP) for _ in range(2)]
        SPT = [sh(SP) for _ in range(2)]
        SPS = sh(SP)
        ATT = [sh(SP) for _ in range(2)]
        YC = sh(SP)
        GQ = sf(2)
        scale = 1.0 / math.sqrt(128.0)
        P.add("vector", lambda e: e.tensor_scalar(out=GQ[:, 0:1], in0=par[:, PO["qn"]:PO["qn"] + 1], scalar1=scale, scalar2=None, op0=ALU.mult),
              reads=["par"], writes=["GQ"])
        P.add("vector", lambda e: e.tensor_copy(out=GQ[:, 1:2], in_=par[:, PO["kn"]:PO["kn"] + 1]), reads=["par"], writes=["GQ"])
        for c in range(8):
            for i in range(NSP):
                t0 = i * SP
                for (j, dstt, dk, gi) in ((5, QN, "QN", 0), (6, KN, "KN", 1)):
                    nb = c * 10 + j
                    ld(LQ, "LQ", proj[nb * 128:(nb + 1) * 128, t0:t0 + SP], ("proj", nb, t0))
                    P.add("vector", lambda e: e.tensor_tensor(out=SQ, in0=LQ, in1=LQ, op=ALU.mult), reads=["LQ"], writes=["SQ"])
                    P.add("tensor", lambda e: e.matmul(PSB[4][:], lhsT=ones_f[:], rhs=SQ, start=True, stop=True), reads=["SQ", "ones_f"], writes=[("ps", 4)])
                    P.add("vector", lambda e: e.tensor_scalar(out=RQ, in0=PSB[4][:], scalar1=1.0 / 128, scalar2=EPS, op0=ALU.mult, op1=ALU.add),
                          reads=[("ps", 4)], writes=["RQ"])
                    P.add("scalar", lambda e: e.activation(out=RQ, in_=RQ, func=AF.Sqrt), reads=["RQ"], writes=["RQ"])
                    P.add("vector", lambda e: e.reciprocal(out=RQ, in_=RQ), reads=["RQ"], writes=["RQ"])
                    P.add("vector", (lambda e, dstt=dstt, gi=gi, t0=t0: e.scalar_tensor_tensor(out=dstt[:, t0:t0 + SP], in0=LQ, scalar=GQ[:, gi:gi + 1], in1=RQ,
                                                                                              op0=ALU.mult, op1=ALU.mult)),
                          reads=["LQ", "RQ", "GQ"], writes=[dk])
                nb = c * 10 + 7
                ld(LQ, "LQ", proj[nb * 128:(nb + 1) * 128, t0:t0 + SP], ("proj", nb, t0))
                for j in range(SP // 128):
                    P.add("tensor", (lambda e, j=j: e.transpose(PT[:, j * 128:(j + 1) * 128], LQ[:, j * 128:(j + 1) * 128], ident_b)),
                          reads=["LQ", "cb16"], writes=["PT"])
                P.add("vector", (lambda e, t0=t0: e.tensor_copy(out=VT[:, t0:t0 + SP], in_=PT[:, 0:SP])), reads=["PT"], writes=["VT"])
            for i in range(NSP):
                t0 = i * SP
                nq = SP // 128
                kbs = list(range(i * nq + nq - 1, -1, -1))
                po = PSB[6]
                for n, kb in enumerate(kbs):
                    r = kb - i * nq
                    a = n % 2
                    pa = PSB[a]; pb = PSB[2 + a]
                    P.add("tensor", (lambda e, pa=pa, kb=kb, t0=t0: e.matmul(pa[:], lhsT=KN[:, kb * 128:(kb + 1) * 128], rhs=QN[:, t0:t0 + SP], start=True, stop=True)),
                          reads=["KN", "QN"], writes=[("ps", a)])
                    P.add("scalar", (lambda e, pa=pa, a=a: e.activation(out=E[a], in_=pa[:], func=AF.Exp)), reads=[("ps", a)], writes=[("E", a)])
                    P.add("scalar", (lambda e, a=a: e.activation(out=SPT[a], in_=E[a], func=AF.Ln, bias=1.0)), reads=[("E", a)], writes=[("SPT", a)])
                    if r >= 0:
                        P.add("gpsimd", (lambda e, a=a, r=r: e.tensor_tensor(out=SPT[a], in0=SPT[a], in1=maskLT(r), op=ALU.mult)),
                              reads=[("SPT", a), "cb16"], writes=[("SPT", a)])
                    P.add("tensor", (lambda e, pb=pb, kb=kb, t0=t0: e.matmul(pb[:], lhsT=KN[:, kb * 128:(kb + 1) * 128], rhs=QN[:, t0:t0 + SP], start=True, stop=False)),
                          reads=["KN", "QN"], writes=[("ps", 2 + a)])
                    P.add("tensor", (lambda e, pb=pb, a=a, n=n: e.matmul(pb[:], lhsT=NU_b, rhs=SPT[a], start=False, stop=(n == 0))),
                          reads=[("SPT", a), "cb16"], writes=[("ps", 2 + a)])
                    if n > 0:
                        P.add("tensor", (lambda e, pb=pb: e.matmul(pb[:], lhsT=NEG1_b, rhs=SPS, start=False, stop=True)),
                              reads=["SPS", "cb16"], writes=[("ps", 2 + a)])
                    P.add("scalar", (lambda e, pb=pb, a=a: e.activation(out=ATT[a], in_=pb[:], func=AF.Exp)), reads=[("ps", 2 + a)], writes=[("ATT", a)])
                    if r >= 0:
                        P.add("gpsimd", (lambda e, a=a, r=r: e.tensor_tensor(out=ATT[a], in0=ATT[a], in1=maskLT(r), op=ALU.mult)),
                              reads=[("ATT", a), "cb16"], writes=[("ATT", a)])
                    P.add("tensor", (lambda e, kb=kb, a=a, n=n: e.matmul(po[:], lhsT=VT[:, kb * 128:(kb + 1) * 128], rhs=ATT[a], start=(n == 0), stop=(n == len(kbs) - 1))),
                          reads=["VT", ("ATT", a)], writes=[("ps", 6)])
                    if n == 0:
                        P.add("gpsimd", (lambda e, a=a: e.tensor_copy(out=SPS, in_=SPT[a])), reads=[("SPT", a)], writes=["SPS"])
                    elif n < len(kbs) - 1:
                        P.add("gpsimd", (lambda e, a=a: e.tensor_tensor(out=SPS, in0=SPS, in1=SPT[a], op=ALU.add)), reads=[("SPT", a), "SPS"], writes=["SPS"])
                P.add("vector", lambda e: e.tensor_copy(out=YC, in_=po[:]), reads=[("ps", 6)], writes=["YC"])
                store("gpsimd", ymix[(16 + c) * 128:(16 + c + 1) * 128, t0:t0 + SP], YC, "YC", ("ymix", 16 + c, t0))

    def mix_d(l, Wl):
        soff[0] = 0
        LXS = sh(SP); LBm = sh(SP); LCm = sh(SP); LDT = sh(SP)
        XE = sf(SP + 3); BE = sf(SP + 3); CE = sf(SP + 3)
        HX = sf(3); HB = sf(3); HC = sf(3)
        XC = sh(SP); BC = sh(SP); CC = sh(SP); ACCD = sf(SP); YD = sh(SP)
        DTB = sf(48); DTR = sf(48); A2 = sf(16)
        DT2 = sf(2); DA = sf(2); CUMC = sf(2); CL = sf(2); DE = sf(2); CD = sf(2); TE = sf(2)
        DAB = [sf(128) for _ in range(2)]
        TD = [sf(128) for _ in range(2)]
        MH = [sh(128) for _ in range(2)]
        ECB = [sf(128) for _ in range(2)]
        CS = [sh(128) for _ in range(2)]
        XDTA = sh(128); XDTB = sh(128); XDE = sh(128); BT = sh(128)
        HF = sf(128); HTA = sh(128); HTB = sh(128)
        DBG = sf(2048) if dbg else None
        P.add("gpsimd", lambda e: e.dma_start(out=DTR[0:1, :], in_=Wl["dtb"][:, :]), writes=["DTR"], dma_key="dtr")
        P.add("tensor", lambda e: e.matmul(PSB[0][:, 0:48], lhsT=ones_f[0:1, :], rhs=DTR[0:1, :], start=True, stop=True), reads=["DTR", "ones_f"], writes=[("ps", 0)])
        P.add("vector", lambda e: e.tensor_copy(out=DTB, in_=PSB[0][:, 0:48]), reads=[("ps", 0)], writes=["DTB"])
        P.add("scalar", lambda e: e.activation(out=A2, in_=DTB[:, 16:32], func=AF.Exp), reads=["DTB"], writes=["A2"])
        P.add("vector", lambda e: e.tensor_scalar(out=A2, in0=A2, scalar1=-1.0, scalar2=None, op0=ALU.mult), reads=["A2"], writes=["A2"])
        P.add("vector", lambda e: e.memset(XDTA, 0.0), writes=["XDTA"])
        P.add("vector", lambda e: e.memset(XDTB, 0.0), writes=["XDTB"])
        for c in range(8):
            g = c // 4
            h0 = 2 * c
            P.add("vector", lambda e: e.memset(HF, 0.0), writes=["HF"])
            P.add("vector", lambda e: e.memset(HTA, 0.0), writes=["HTA"])
            P.add("vector", lambda e: e.memset(HTB, 0.0), writes=["HTB"])
            for i in range(NSP):
                t0 = i * SP
                srcs = ((LXS, "LXS", c * 10 + 9, XE, "XE", HX, "HX", XC, "XC", PO["cw_x"] + 4 * c, PO["cb_x"] + c),
                        (LBm, "LBm", 80 + 2 * g, BE, "BE", HB, "HB", BC, "BC", PO["cw_bc"] + 4 * (2 * g), PO["cb_bc"] + 2 * g),
                        (LCm, "LCm", 81 + 2 * g, CE, "CE", HC, "HC", CC, "CC", PO["cw_bc"] + 4 * (2 * g + 1), PO["cb_bc"] + 2 * g + 1))
                for (LT, lk, nb, EXT, ek, HL, hk, OUT, ok, wc, bc) in srcs:
                    ld(LT, lk, proj[nb * 128:(nb + 1) * 128, t0:t0 + SP], ("proj", nb, t0))
                    if i == 0:
                        P.add("vector", (lambda e, EXT=EXT: e.memset(EXT[:, 0:3], 0.0)), writes=[ek])
                    else:
                        P.add("vector", (lambda e, EXT=EXT, HL=HL: e.tensor_copy(out=EXT[:, 0:3], in_=HL)), reads=[hk], writes=[ek])
                    P.add("vector", (lambda e, EXT=EXT, LT=LT: e.tensor_copy(out=EXT[:, 3:3 + SP], in_=LT)), reads=[lk], writes=[ek])
                    P.add("vector", (lambda e, EXT=EXT, wc=wc: e.tensor_scalar(out=ACCD, in0=EXT[:, 0:SP], scalar1=par[:, wc:wc + 1], scalar2=None, op0=ALU.mult)),
                          reads=[ek, "par"], writes=["ACCD"])
                    for k in (1, 2, 3):
                        P.add("vector", (lambda e, EXT=EXT, wc=wc, k=k: e.scalar_tensor_tensor(out=ACCD, in0=EXT[:, k:k + SP], scalar=par[:, wc + k:wc + k + 1],
                                                                                               in1=ACCD, op0=ALU.mult, op1=ALU.add)),
                              reads=[ek, "par", "ACCD"], writes=["ACCD"])
                    P.add("vector", (lambda e, EXT=EXT, HL=HL: e.tensor_copy(out=HL, in_=EXT[:, SP:SP + 3])), reads=[ek], writes=[hk])
                    P.add("scalar", (lambda e, OUT=OUT, bc=bc: e.activation(out=OUT, in_=ACCD, func=AF.Silu, bias=par[:, bc:bc + 1])),
                          reads=["ACCD", "par"], writes=[ok])
                ld(LDT, "LDT", proj[84 * 128:85 * 128, t0:t0 + SP], ("proj", 84, t0))
                for j in range(SP // 128):
                    cs = slice(j * 128, (j + 1) * 128)
                    P.add("tensor", (lambda e, cs=cs: e.transpose(PT[:, 0:128], LDT[:, cs], ident_b)), reads=["LDT", "cb16"], writes=[("PT", 0)])
                    P.add("vector", (lambda e, h0=h0: e.tensor_tensor(out=DT2, in0=PT[:, h0:h0 + 2], in1=DTB[:, h0:h0 + 2], op=ALU.add)),
                          reads=[("PT", 0), "DTB"], writes=["DT2"])
                    P.add("scalar", lambda e: e.activation(out=DT2, in_=DT2, func=AF.Exp), reads=["DT2"], writes=["DT2"])
                    P.add("scalar", lambda e: e.activation(out=DT2, in_=DT2, func=AF.Ln, bias=1.0), reads=["DT2"], writes=["DT2"])
                    P.add("vector", (lambda e, h0=h0: e.tensor_tensor(out=DA, in0=DT2, in1=A2[:, h0:h0 + 2], op=ALU.mult)), reads=["DT2", "A2"], writes=["DA"])
                    for h in range(2):
                        P.add("vector", (lambda e, h=h: e.tensor_scalar(out=DAB[h], in0=ones_f[:], scalar1=DA[:, h:h + 1], scalar2=None, op0=ALU.mult)),
                              reads=["DA", "ones_f"], writes=[("DAB", h)])
                        P.add("tensor", (lambda e, h=h: e.matmul(PSB[h][:, 0:128], lhsT=DAB[h], rhs=LE_f, start=True, stop=True)),
                              reads=[("DAB", h), "cf32"], writes=[("ps", h)])
                    P.add("tensor", lambda e: e.matmul(PSB[2][:, 0:2], lhsT=LE_f, rhs=DA, start=True, stop=True), reads=["DA", "cf32"], writes=[("ps", 2)])
                    P.add("vector", lambda e: e.tensor_copy(out=CUMC, in_=PSB[2][:, 0:2]), reads=[("ps", 2)], writes=["CUMC"])
                    for h in range(2):
                        P.add("vector", (lambda e, h=h: e.tensor_copy(out=CL[:, h:h + 1], in_=PSB[h][:, 127:128])), reads=[("ps", h)], writes=["CL"])
                    P.add("tensor", (lambda e, cs=cs: e.matmul(PSB[3][:, 0:128], lhsT=BC[:, cs], rhs=CC[:, cs], start=True, stop=True)),
                          reads=["BC", "CC"], writes=[("ps", 3)])
                    P.add("tensor", (lambda e, cs=cs: e.transpose(PT[:, 128:256], XC[:, cs], ident_b)), reads=["XC", "cb16"], writes=[("PT", 1)])
                    P.add("vector", lambda e: e.tensor_scalar(out=XDTA[:, 0:64], in0=PT[:, 128:192], scalar1=DT2[:, 0:1], scalar2=None, op0=ALU.mult),
                          reads=[("PT", 1), "DT2"], writes=["XDTA"])
                    P.add("vector", lambda e: e.tensor_scalar(out=XDTB[:, 64:128], in0=PT[:, 192:256], scalar1=DT2[:, 1:2], scalar2=None, op0=ALU.mult),
                          reads=[("PT", 1), "DT2"], writes=["XDTB"])
                    for h in range(2):
                        P.add("vector", (lambda e, h=h: e.tensor_scalar(out=TD[h], in0=PSB[h][:, 0:128], scalar1=CUMC[:, h:h + 1], scalar2=0.0,
                                                                        op0=ALU.subtract, op1=ALU.min)),
                              reads=[("ps", h), "CUMC"], writes=[("TD", h)])
                        P.add("scalar", (lambda e, h=h: e.activation(out=TD[h], in_=TD[h], func=AF.Exp)), reads=[("TD", h)], writes=[("TD", h)])
                        P.add("vector", (lambda e, h=h: e.tensor_tensor(out=TD[h], in0=TD[h], in1=LE_f, op=ALU.mult)), reads=[("TD", h), "cf32"], writes=[("TD", h)])
                        P.add("vector", (lambda e, h=h: e.tensor_tensor(out=MH[h], in0=PSB[3][:, 0:128], in1=TD[h], op=ALU.mult)),
                              reads=[("ps", 3), ("TD", h)], writes=[("MH", h)])
                        P.add("scalar", (lambda e, h=h: e.activation(out=ECB[h], in_=PSB[h][:, 0:128], func=AF.Exp)), reads=[("ps", h)], writes=[("ECB", h)])
                        P.add("vector", (lambda e, h=h, cs=cs: e.tensor_tensor(out=CS[h], in0=CC[:, cs], in1=ECB[h], op=ALU.mult)),
                              reads=["CC", ("ECB", h)], writes=[("CS", h)])
                    P.add("tensor", lambda e: e.matmul(PSB[4][:, 0:128], lhsT=XDTA, rhs=MH[0], start=True, stop=False), reads=["XDTA", ("MH", 0)], writes=[("ps", 4)])
                    P.add("tensor", lambda e: e.matmul(PSB[4][:, 0:128], lhsT=XDTB, rhs=MH[1], start=False, stop=False), reads=["XDTB", ("MH", 1)], writes=[("ps", 4)])
                    P.add("tensor", lambda e: e.matmul(PSB[4][:, 0:128], lhsT=HTA, rhs=CS[0], start=False, stop=False), reads=["HTA", ("CS", 0)], writes=[("ps", 4)])
                    P.add("tensor", lambda e: e.matmul(PSB[4][:, 0:128], lhsT=HTB, rhs=CS[1], start=False, stop=True), reads=["HTB", ("CS", 1)], writes=[("ps", 4)])
                    P.add("vector", (lambda e, cs=cs, c=c: e.scalar_tensor_tensor(out=YD[:, cs], in0=XC[:, cs], scalar=par[:, PO["dskip"] + c:PO["dskip"] + c + 1],
                                                                                   in1=PSB[4][:, 0:128], op0=ALU.mult, op1=ALU.add)),
                          reads=["XC", "par", ("ps", 4)], writes=["YD"])
                    for h in range(2):
                        P.add("scalar", (lambda e, h=h: e.activation(out=DE[:, h:h + 1], in_=CUMC[:, h:h + 1], func=AF.Exp, scale=-1.0, bias=CL[:, h:h + 1])),
                              reads=["CUMC", "CL"], writes=["DE"])
                    P.add("scalar", lambda e: e.activation(out=CD, in_=CL, func=AF.Exp), reads=["CL"], writes=["CD"])
                    P.add("vector", lambda e: e.tensor_scalar(out=XDE[:, 0:64], in0=XDTA[:, 0:64], scalar1=DE[:, 0:1], scalar2=None, op0=ALU.mult),
                          reads=["XDTA", "DE"], writes=["XDE"])
                    P.add("vector", lambda e: e.tensor_scalar(out=XDE[:, 64:128], in0=XDTB[:, 64:128], scalar1=DE[:, 1:2], scalar2=None, op0=ALU.mult),
                          reads=["XDTB", "DE"], writes=["XDE"])
                    P.add("tensor", (lambda e, cs=cs: e.transpose(PT[:, 256:384], BC[:, cs], ident_b)), reads=["BC", "cb16"], writes=[("PT", 2)])
                    P.add("vector", lambda e: e.tensor_copy(out=BT, in_=PT[:, 256:384]), reads=[("PT", 2)], writes=["BT"])
                    P.add("tensor", lambda e: e.matmul(PSB[5][:, 0:128], lhsT=BT, rhs=XDE, start=True, stop=True), reads=["BT", "XDE"], writes=[("ps", 5)])
                    P.add("vector", lambda e: e.scalar_tensor_tensor(out=HF[:, 0:64], in0=HF[:, 0:64], scalar=CD[:, 0:1], in1=PSB[5][:, 0:64], op0=ALU.mult, op1=ALU.add),
                          reads=["HF", "CD", ("ps", 5)], writes=["HF"])
                    P.add("vector", lambda e: e.scalar_tensor_tensor(out=HF[:, 64:128], in0=HF[:, 64:128], scalar=CD[:, 1:2], in1=PSB[5][:, 64:128], op0=ALU.mult, op1=ALU.add),
                          reads=["HF", "CD", ("ps", 5)], writes=["HF"])
                    P.add("vector", lambda e: e.tensor_copy(out=HTA[:, 0:64], in_=HF[:, 0:64]), reads=["HF"], writes=["HTA"])
                    P.add("vector", lambda e: e.tensor_copy(out=HTB[:, 64:128], in_=HF[:, 64:128]), reads=["HF"], writes=["HTB"])
                    if dbg and c == 0 and i == 0 and j == 0:
                        items = [(DTB, 48, "DTB"), (A2, 16, "A2"), (DT2, 2, "DT2"), (DA, 2, "DA"), (CUMC, 2, "CUMC"), (CL, 2, "CL"), (DE, 2, "DE"), (CD, 2, "CD"),
                                 (TD[0], 128, ("TD", 0)), (MH[0], 128, ("MH", 0)), (ECB[0], 128, ("ECB", 0)), (CS[0], 128, ("CS", 0)),
                                 (XDTA, 128, "XDTA"), (XC[:, 0:128], 128, "XC"), (BC[:, 0:128], 128, "BC"), (CC[:, 0:128], 128, "CC"), (HF, 128, "HF"),
                                 (YD[:, 0:128], 128, "YD"), (XDE, 128, "XDE"), (BT, 128, "BT")]
                        o = 0
                        for (ap_, w_, k_) in items:
                            P.add("vector", (lambda e, ap_=ap_, o=o, w_=w_: e.tensor_copy(out=DBG[:, o:o + w_], in_=ap_)), reads=[k_], writes=["DBG"])
                            print("DBGITEM", k_, o, w_)
                            o += w_
                        P.add("gpsimd", lambda e: e.dma_start(out=dbg_d[:, :], in_=DBG), reads=["DBG"], writes=["dbg_d"], dma_key="dbg")
                store("gpsimd", ymix[(24 + c) * 128:(25 + c) * 128, t0:t0 + SP], YD, "YD", ("ymix", 24 + c, t0))

    def phase3(l, Wl, src, dst):
        soff[0] = 0
        cast_weights(l, ["w_br", "w_out", "w_g", "w_u", "w_d"])
        YG = [sf(TB) for _ in range(4)]
        ZT = sh(TB); YT = sh(TB); T1 = sf(TB); T2 = sf(TB); R3 = sf(TB)
        macc = sf(TB)
        assert soff[0] * 2 <= 20 * TB
        for tb in range(NTB):
            t0 = tb * TB
            for m in range(3):
                for c in range(8):
                    kk = m * 8 + c
                    P.add("gpsimd", (lambda e, kk=kk, t0=t0: e.dma_start(out=HT[:, kk, :], in_=ymix[kk * 128:(kk + 1) * 128, t0:t0 + TB])),
                          reads=[("ymix", kk, t0)], writes=[("HT", kk)], dma_key=("ldht", kk))
            for g in range(2):
                for q in range(4):
                    c = g * 4 + q
                    ld(YT, "YT", ymix[(24 + c) * 128:(25 + c) * 128, t0:t0 + TB], ("ymix", 24 + c, t0))
                    nb = c * 10 + 8
                    ld(ZT, "ZT", proj[nb * 128:(nb + 1) * 128, t0:t0 + TB], ("proj", nb, t0))
                    P.add("scalar", lambda e: e.activation(out=T1, in_=ZT, func=AF.Silu), reads=["ZT"], writes=["T1"])
                    P.add("vector", (lambda e, q=q: e.tensor_tensor(out=YG[q], in0=YT, in1=T1, op=ALU.mult)), reads=["YT", "T1"], writes=[("YG", q)])
                    P.add("scalar", (lambda e, q=q: e.activation(out=T2, in_=YG[q], func=AF.Square)), reads=[("YG", q)], writes=["T2"])
                    P.add("tensor", (lambda e, q=q: e.matmul(PSB[4][:], lhsT=ones_f[:], rhs=T2, start=(q == 0), stop=(q == 3))),
                          reads=["T2", "ones_f"], writes=[("ps", 4)])
                P.add("vector", lambda e: e.tensor_scalar(out=R3, in0=PSB[4][:], scalar1=1.0 / 512, scalar2=EPS, op0=ALU.mult, op1=ALU.add),
                      reads=[("ps", 4)], writes=["R3"])
                P.add("scalar", lambda e: e.activation(out=R3, in_=R3, func=AF.Sqrt), reads=["R3"], writes=["R3"])
                P.add("vector", lambda e: e.reciprocal(out=R3, in_=R3), reads=["R3"], writes=["R3"])
                for q in range(4):
                    c = g * 4 + q
                    P.add("vector", (lambda e, q=q, c=c: e.scalar_tensor_tensor(out=HT[:, 24 + c, :], in0=YG[q], scalar=par[:, PO["ssm_norm"] + c:PO["ssm_norm"] + c + 1],
                                                                                 in1=R3, op0=ALU.mult, op1=ALU.mult)),
                          reads=[("YG", q), "R3", "par"], writes=[("HT", 24 + c)])
            wb = Wl["w_br_b"]
            for pi in range(D // NP):
                slot = rot("wp", 3); wp = WP[slot]
                P.add("sync", (lambda e, wp=wp, pi=pi: e.dma_start(out=wp[:, 0:KC * NP], in_=wb[pi])),
                      reads=[("w_br_b", l, pi)], writes=[("wp", slot)], dma_key=("wp", slot))
                for j in range(NP // 128):
                    nb = pi * (NP // 128) + j
                    for m in range(4):
                        b = rot("ps", 4); pt = PSB[b]
                        for c in range(8):
                            kk = m * 8 + c
                            P.add("tensor", (lambda e, pt=pt, wp=wp, kk=kk, j=j, c=c: e.matmul(
                                pt[:], lhsT=wp[:, kk * NP + j * 128:kk * NP + (j + 1) * 128], rhs=HT[:, kk, :], start=(c == 0), stop=(c == 7))),
                                reads=[("wp", slot), ("HT", kk)], writes=[("ps", b)])
                        gi = rot("gt", 6)
                        gr = m * 32 + nb
                        P.add("gpsimd", (lambda e, gi=gi, gr=gr, t0=t0: e.dma_start(out=GT[gi][:], in_=gates[gr * 128:(gr + 1) * 128, t0:t0 + TB])),
                              reads=[("gates", gr, t0)], writes=[("gt", gi)], dma_key=("gt", gi))
                        if m == 0:
                            P.add("vector", (lambda e, pt=pt, gi=gi: e.tensor_tensor(out=macc, in0=pt[:], in1=GT[gi][:], op=ALU.mult)),
                                  reads=[("ps", b), ("gt", gi)], writes=["macc"])
                        else:
                            P.add("vector", (lambda e, pt=pt, gi=gi: e.tensor_tensor(out=T1, in0=pt[:], in1=GT[gi][:], op=ALU.mult)),
                                  reads=[("ps", b), ("gt", gi)], writes=["T1"])
                            if m < 3:
                                P.add("vector", lambda e: e.tensor_tensor(out=macc, in0=macc, in1=T1, op=ALU.add), reads=["macc", "T1"], writes=["macc"])
                            else:
                                P.add("vector", (lambda e, nb=nb: e.tensor_tensor(out=BIGb[:, (20 + nb) * TB:(20 + nb + 1) * TB], in0=macc, in1=T1, op=ALU.add)),
                                      reads=["macc", "T1"], writes=[("BIG", 20 + nb)])

            def epi_o(nb, pt, pkey, t0=t0):
                i = rot("stf", 5)
                P.add("gpsimd", (lambda e: e.dma_start(out=STF[i][:], in_=src[nb * 128:(nb + 1) * 128, t0:t0 + TB])), writes=[("stf", i)], dma_key=("stfl", i))
                P.add("vector", (lambda e: e.tensor_tensor(out=STF[i][:], in0=pt[:], in1=STF[i][:], op=ALU.add)), reads=[pkey, ("stf", i)], writes=[("stf", i)])
                store("gpsimd", x1[nb * 128:(nb + 1) * 128, t0:t0 + TB], STF[i][:], ("stf", i), ("x1", nb, t0))
            gemm("w_out", l, (lambda kc: BIGb[:, (20 + kc) * TB:(20 + kc + 1) * TB]), (lambda kc: ("BIG", 20 + kc)), KC, NP, D // NP, epi_o)
            if stop_after == "p3":
                P.barrier()
                continue
            rmsnorm_block(x1, tb, PO["norm_ffn"], xkey=lambda kc, t0=t0: ("x1", kc, t0))
            wg = Wl["w_g_b"]; wu = Wl["w_u_b"]
            for pi in range(DFF // NP):
                sg = rot("wp", 3); su = rot("wp", 3)
                P.add("sync", (lambda e, sg=sg, pi=pi: e.dma_start(out=WP[sg][:, 0:KC * NP], in_=wg[pi])), reads=[("w_g_b", l, pi)], writes=[("wp", sg)], dma_key=("wp", sg))
                P.add("sync", (lambda e, su=su, pi=pi: e.dma_start(out=WP[su][:, 0:KC * NP], in_=wu[pi])), reads=[("w_u_b", l, pi)], writes=[("wp", su)], dma_key=("wp", su))
                for j in range(NP // 128):
                    hb = pi * (NP // 128) + j
                    bg = rot("ps", 4); bu = rot("ps", 4)
                    for (slot, b) in ((sg, bg), (su, bu)):
                        for kc in range(KC):
                            P.add("tensor", (lambda e, slot=slot, b=b, kc=kc, j=j: e.matmul(
                                PSB[b][:], lhsT=WP[slot][:, kc * NP + j * 128:kc * NP + (j + 1) * 128], rhs=HT[:, kc, :], start=(kc == 0), stop=(kc == KC - 1))),
                                reads=[("wp", slot), "HT"], writes=[("ps", b)])
                    i = rot("stf", 5)
                    P.add("scalar", (lambda e, i=i, bg=bg: e.activation(out=STF[i][:], in_=PSB[bg][:], func=AF.Silu)), reads=[("ps", bg)], writes=[("stf", i)])
                    P.add("vector", (lambda e, i=i, bu=bu, hb=hb: e.tensor_tensor(out=BIGb[:, hb * TB:(hb + 1) * TB], in0=PSB[bu][:], in1=STF[i][:], op=ALU.mult)),
                          reads=[("ps", bu), ("stf", i)], writes=[("BIG", hb)])

            def epi_d(nb, pt, pkey, t0=t0):
                i = rot("stf", 5)
                P.add("gpsimd", (lambda e: e.dma_start(out=STF[i][0:64, :], in_=x1[nb * 64:(nb + 1) * 64, t0:t0 + TB])),
                      reads=[("x1", nb // 2, t0)], writes=[("stf", i)], dma_key=("stfl", i))
                P.add("vector", (lambda e: e.tensor_tensor(out=STF[i][0:64, :], in0=pt[0:64, :], in1=STF[i][0:64, :], op=ALU.add)), reads=[pkey, ("stf", i)], writes=[("stf", i)])
                store("gpsimd", dst[nb * 64:(nb + 1) * 64, t0:t0 + TB], STF[i][0:64, :], ("stf", i), ("dst", nb, t0))
            gemm("w_d", l, (lambda kc: BIGb[:, kc * TB:(kc + 1) * TB]), (lambda kc: ("BIG", kc)), KF, 64, D // 64, epi_d)
            P.barrier()

    final = []
    for l in range(depth):
        src = xin if l == 0 else x2
        dst = yout if l == depth - 1 else x2
        Wl = W[l]
        P.add("sync", lambda e, Wl=Wl: e.dma_start(out=par[:], in_=Wl["par"][:, :]), writes=["par"], dma_key="par")
        cast_weights(l, ["w_in"])

        for tb in range(NTB):
            t0 = tb * TB
            rmsnorm_block(src, tb, PO["norm_mix"])

            def epi(nb, pt, pkey, t0=t0):
                i = rot("stg", 4)
                if nb < NB_MIX:
                    P.add("vector", (lambda e: e.tensor_copy(out=STG[i][:], in_=pt[:])), reads=[pkey], writes=[("stg", i)])
                    store("gpsimd", proj[nb * 128:(nb + 1) * 128, t0:t0 + TB], STG[i][:], ("stg", i), ("proj", nb, t0))
                else:
                    g = nb - NB_MIX
                    P.add("scalar", (lambda e: e.activation(out=STG[i][:], in_=pt[:], func=AF.Sigmoid)), reads=[pkey], writes=[("stg", i)])
                    store("gpsimd", gates[g * 128:(g + 1) * 128, t0:t0 + TB], STG[i][:], ("stg", i), ("gates", g, t0))
            gemm("w_in", l, (lambda kc: HT[:, kc, :]), "HT", KC, NP, NPI, epi)
        if stop_after == "p1":
            break
        P.barrier()
        SP = TB
        NSP = S // SP
        V = mybir

        def gelu(X, G, T):
            return [("vector", lambda e: e.tensor_tensor(out=T, in0=X, in1=X, op=ALU.mult)),
                    ("vector", lambda e: e.tensor_scalar(out=T, in0=T, scalar1=0.044715, scalar2=1.0, op0=ALU.mult, op1=ALU.add)),
                    ("vector", lambda e: e.tensor_tensor(out=T, in0=T, in1=X, op=ALU.mult)),
                    ("scalar", lambda e: e.activation(out=T, in_=T, func=AF.Sigmoid, scale=1.5957691216057308)),
                    ("vector", lambda e: e.tensor_tensor(out=G, in0=X, in1=T, op=ALU.mult))]

        def ld(dst, dkey, src_ap, skey, q="gpsimd"):
            return P.add(q, (lambda e: e.dma_start(out=dst, in_=src_ap)), reads=[skey], writes=[dkey], dma_key=("ld", dkey))

        soff[0] = 0
        GV = [sf(TB) for _ in range(8)]
        VB = [sh(TB) for _ in range(8)]
        TMP = sf(TB); MEAN = sf(TB); RST = sf(TB)
        for tb in range(NTB):
            t0 = tb * TB
            for c in range(8):
                nb = c * 10 + 4
                ld(VB[c], ("VB", c), proj[nb * 128:(nb + 1) * 128, t0:t0 + TB], ("proj", nb, t0))
                for (en, fn) in gelu(VB[c], GV[c], TMP):
                    P.add(en, fn, reads=[("VB", c), ("GV", c), "TMP"], writes=[("GV", c), "TMP"])
                P.add("tensor", (lambda e, c=c: e.matmul(PSB[4][:], lhsT=ones_f[:], rhs=GV[c], start=(c == 0), stop=(c == 7))),
                      reads=[("GV", c), "ones_f"], writes=[("ps", 4)])
                P.add("scalar", (lambda e, c=c: e.activation(out=TMP, in_=GV[c], func=AF.Square)), reads=[("GV", c)], writes=["TMP"])
                P.add("tensor", (lambda e, c=c: e.matmul(PSB[5][:], lhsT=ones_f[:], rhs=TMP, start=(c == 0), stop=(c == 7))),
                      reads=["TMP", "ones_f"], writes=[("ps", 5)])
            P.add("vector", lambda e: e.tensor_scalar(out=MEAN, in0=PSB[4][:], scalar1=1.0 / WM, scalar2=None, op0=ALU.mult), reads=[("ps", 4)], writes=["MEAN"])
            P.add("vector", lambda e: e.tensor_tensor(out=TMP, in0=MEAN, in1=MEAN, op=ALU.mult), reads=["MEAN"], writes=["TMP"])
            P.add("vector", lambda e: e.scalar_tensor_tensor(out=RST, in0=PSB[5][:], scalar=1.0 / WM, in1=TMP, op0=ALU.mult, op1=ALU.subtract),
                  reads=[("ps", 5), "TMP"], writes=["RST"])
            P.add("vector", lambda e: e.tensor_scalar(out=RST, in0=RST, scalar1=EPS, scalar2=None, op0=ALU.add), reads=["RST"], writes=["RST"])
            P.add("scalar", lambda e: e.activation(out=RST, in_=RST, func=AF.Sqrt), reads=["RST"], writes=["RST"])
            P.add("vector", lambda e: e.reciprocal(out=RST, in_=RST), reads=["RST"], writes=["RST"])
            for c in range(8):
                nb = c * 10 + 4
                P.add("vector", (lambda e, c=c: e.tensor_tensor(out=GV[c], in0=GV[c], in1=MEAN, op=ALU.subtract)), reads=[("GV", c), "MEAN"], writes=[("GV", c)])
                P.add("vector", (lambda e, c=c: e.scalar_tensor_tensor(out=VB[c], in0=GV[c], scalar=par[:, PO["sgu_norm"] + c:PO["sgu_norm"] + c + 1],
                                                                        in1=RST, op0=ALU.mult, op1=ALU.mult)),
                      reads=[("GV", c), "RST", "par"], writes=[("VB", c)])
                P.add("gpsimd", (lambda e, c=c, nb=nb, t0=t0: e.dma_start(out=proj[nb * 128:(nb + 1) * 128, t0:t0 + TB], in_=VB[c])),
                      reads=[("VB", c)], writes=[("proj", nb, t0)], dma_key=("st", ("VB", c)))
        P.barrier()
        if stop_after == "p1b":
            break

        soff[0] = 0
        WBT = sh(8 * 128)
        BHI = sh(1024); BLO = sh(1024)
        BF = sf(1024); BF2 = sf(1024)
        LB = sh(SP); LC = sh(SP); LX = sh(SP); LU = sh(SP); LV = sh(SP)
        CXE = sf(SP + 2); HAL = sf(2); ACC = sf(SP); YA = sh(SP)
        GU = sf(SP); TG = sf(SP); VT_ = sh(128); YB = sh(SP)
        for c in range(8):
            P.add("gpsimd", (lambda e, c=c, Wl=Wl: e.dma_start(out=TG[:, 0:128], in_=Wl["sguw"][c])), writes=["TG"], dma_key="sguw")
            P.add("vector", (lambda e, c=c: e.tensor_tensor(out=WBT[:, c * 128:(c + 1) * 128], in0=TG[:, 0:128], in1=LE_f, op=ALU.mult)),
                  reads=["TG", "cf32"], writes=["WBT"])
        P.add("gpsimd", lambda e, Wl=Wl: e.dma_start(out=BF[0:1, :], in_=Wl["sgub"][:, :]), writes=["BF"], dma_key="sgub")
        P.add("vector", lambda e: e.tensor_copy(out=BHI[0:1, :], in_=BF[0:1, :]), reads=["BF"], writes=["BHI"])
        P.add("vector", lambda e: e.tensor_copy(out=BF2[0:1, :], in_=BHI[0:1, :]), reads=["BHI"], writes=["BF2"])
        P.add("vector", lambda e: e.tensor_tensor(out=BF2[0:1, :], in0=BF[0:1, :], in1=BF2[0:1, :], op=ALU.subtract), reads=["BF", "BF2"], writes=["BF2"])
        P.add("vector", lambda e: e.tensor_copy(out=BLO[0:1, :], in_=BF2[0:1, :]), reads=["BF2"], writes=["BLO"])
        for c in range(8):
            ca = PO["conv_a"] + 3 * c
            for i in range(NSP):
                t0 = i * SP
                for (tl, k, j) in ((LB, "LB", 0), (LC, "LC", 1), (LX, "LX", 2)):
                    nb = c * 10 + j
                    ld(tl, k, proj[nb * 128:(nb + 1) * 128, t0:t0 + SP], ("proj", nb, t0))
                if i == 0:
                    P.add("vector", lambda e: e.memset(CXE[:, 0:2], 0.0), writes=["CXE"])
                else:
                    P.add("vector", lambda e: e.tensor_copy(out=CXE[:, 0:2], in_=HAL), reads=["HAL"], writes=["CXE"])
                P.add("vector", lambda e: e.tensor_tensor(out=CXE[:, 2:2 + SP], in0=LC, in1=LX, op=ALU.mult), reads=["LC", "LX"], writes=["CXE"])
                P.add("vector", (lambda e, ca=ca: e.tensor_scalar(out=ACC, in0=CXE[:, 0:SP], scalar1=par[:, ca:ca + 1], scalar2=None, op0=ALU.mult)),
                      reads=["CXE", "par"], writes=["ACC"])
                for k in (1, 2):
                    P.add("vector", (lambda e, ca=ca, k=k: e.scalar_tensor_tensor(out=ACC, in0=CXE[:, k:k + SP], scalar=par[:, ca + k:ca + k + 1],
                                                                                   in1=ACC, op0=ALU.mult, op1=ALU.add)),
                          reads=["CXE", "par", "ACC"], writes=["ACC"])
                P.add("vector", lambda e: e.tensor_copy(out=HAL, in_=CXE[:, SP:SP + 2]), reads=["CXE"], writes=["HAL"])
                P.add("vector", lambda e: e.tensor_tensor(out=YA, in0=ACC, in1=LB, op=ALU.mult), reads=["ACC", "LB"], writes=["YA"])
                store("gpsimd", ymix[(0 * 8 + c) * 128:(0 * 8 + c + 1) * 128, t0:t0 + SP], YA, "YA", ("ymix", c, t0))
                ld(LU, "LU", proj[(c * 10 + 3) * 128:(c * 10 + 4) * 128, t0:t0 + SP], ("proj", c * 10 + 3, t0))
                ld(LV, "LV", proj[(c * 10 + 4) * 128:(c * 10 + 5) * 128, t0:t0 + SP], ("proj", c * 10 + 4, t0))
                for (en, fn) in gelu(LU, GU, TG):
                    P.add(en, fn, reads=["LU", "GU", "TG"], writes=["GU", "TG"])
                for j in range(SP // 128):
                    P.add("tensor", (lambda e, j=j: e.transpose(PT[:, 0:128], LV[:, j * 128:(j + 1) * 128], ident_b)),
                          reads=["LV", "cb16"], writes=["PT"])
                    P.add("vector", lambda e: e.tensor_copy(out=VT_, in_=PT[:, 0:128]), reads=["PT"], writes=["VT_"])
                    P.add("tensor", (lambda e, c=c: e.matmul(PSB[4][:, 0:128], lhsT=VT_, rhs=WBT[:, c * 128:(c + 1) * 128], start=True, stop=False)),
                          reads=["VT_", "WBT"], writes=[("ps", 4)])
                    P.add("tensor", (lambda e, c=c: e.matmul(PSB[4][:, 0:128], lhsT=ones_b[0:1, :], rhs=BHI[0:1, c * 128:(c + 1) * 128], start=False, stop=False)),
                          reads=["BHI", "ones_b"], writes=[("ps", 4)])
                    P.add("tensor", (lambda e, c=c: e.matmul(PSB[4][:, 0:128], lhsT=ones_b[0:1, :], rhs=BLO[0:1, c * 128:(c + 1) * 128], start=False, stop=True)),
                          reads=["BLO", "ones_b"], writes=[("ps", 4)])
                    P.add("vector", (lambda e, j=j: e.tensor_tensor(out=YB[:, j * 128:(j + 1) * 128], in0=PSB[4][:, 0:128], in1=GU[:, j * 128:(j + 1) * 128], op=ALU.mult)),
                          reads=[("ps", 4), "GU"], writes=["YB"])
                store("gpsimd", ymix[(1 * 8 + c) * 128:(1 * 8 + c + 1) * 128, t0:t0 + SP], YB, "YB", ("ymix", 8 + c, t0))
        P.barrier()
        mix_c(l, Wl)
        P.barrier()
        mix_d(l, Wl)
        P.barrier()
        if stop_after == "p2":
            break
        phase3(l, Wl, src, dst)
        P.barrier()
    fw = [op for k, op in P.last_w.items() if op.is_dma]
    with nc.Block() as block:
        P.emit(block, fw)
    return nc, es


def prep_weights(l, inp):
    perm = in_col_perm()
    w_in = inp["w_in"][l]
    wp = np.zeros((D, len(perm)), np.float32)
    m = perm >= 0
    wp[:, m] = w_in[:, perm[m]]
    d = {"w_in%d" % l: panelize(wp, NP)}
    del wp
    if "w_branch" in inp:
        d["w_br%d" % l] = panelize(inp["w_branch"][l].reshape(D, D), NP)
        d["w_out%d" % l] = panelize(inp["w_out"][l], NP)
    if "w_ffn_gate" in inp:
        d["w_g%d" % l] = panelize(inp["w_ffn_gate"][l], NP)
        d["w_u%d" % l] = panelize(inp["w_ffn_up"][l], NP)
        d["w_d%d" % l] = panelize(inp["w_ffn_down"][l], 64)
    d.update(small_inputs(l, inp))
    return d


def kernel(**inputs):
    inp = {k: np.asarray(v) for k, v in inputs.items()}
    x = inp["x"]
    Bsz, S, _ = x.shape
    depth = inp["w_in"].shape[0]
    DFF = inp["w_ffn_gate"].shape[2]
    nc, es = build_nc(S, DFF, depth)
    shared = {"consts": build_consts()}
    for l in range(depth):
        shared.update(prep_weights(l, inp))
    in_maps = []
    for b in range(Bsz):
        d = dict(shared)
        d["xT"] = np.ascontiguousarray(x[b].T)
        in_maps.append(d)
    res = run_bass_kernel_spmd(nc, in_maps, core_ids=list(range(Bsz)))
    out = np.stack([np.ascontiguousarray(res.results[b]["yT"].T) for b in range(Bsz)], axis=0)
    return out.astype(np.float32)
```
